# Optimizing a Trainium2 kernel written in Bass

```python
import math
import jax, jax.numpy as jnp
from jax import lax
import numpy as np

D_MODEL = 4096
BATCH = 4
SEQ = 2048
DEPTH = 2
DEC_BATCH = 8
DEC_SEQ = 2048
PAST_LEN = 128

GRID_W = 64
HEAD_DIM = 128
NA_HEADS = 12
NA_ROWS = 8
NA_COLS = 16
SW_HEADS = 12
SW_KV_HEADS = 4
SW_WINDOW = 128
SW_BLOCK = 128
T5_BUCKETS = 32
T5_MAX_DIST = 128
MEM_TOKENS = 256
MEM_HEADS = 4
MEM_HEAD_DIM = 256
NA_WIDTH = NA_HEADS * HEAD_DIM
SW_WIDTH = SW_HEADS * HEAD_DIM
SW_KV_WIDTH = SW_KV_HEADS * HEAD_DIM
MEM_WIDTH = MEM_HEADS * MEM_HEAD_DIM
IN_SPLITS = (NA_WIDTH, NA_WIDTH, NA_WIDTH, NA_WIDTH,
             SW_WIDTH, SW_KV_WIDTH, SW_KV_WIDTH, SW_WIDTH,
             MEM_WIDTH, MEM_WIDTH,
             D_MODEL, D_MODEL, D_MODEL)
IN_WIDTH = sum(IN_SPLITS)
RMS_EPS = 1e-6
NEG_INF = -1e30

kernel_name = 'hybrid_natten_swa_memory_encoder'


def rms_norm(x, g):
    xf = x.astype(jnp.float32)
    y = xf * lax.rsqrt(jnp.mean(xf * xf, axis=-1, keepdims=True) + RMS_EPS)
    return (y * g.astype(jnp.float32)).astype(x.dtype)


def t5_buckets(rel):
    nb = T5_BUCKETS // 2
    max_exact = nb // 2
    ret = (rel > 0).astype(np.int32) * nb
    n = np.abs(rel)
    large = max_exact + (np.log(np.maximum(n, 1) / max_exact)
                         / np.log(T5_MAX_DIST / max_exact) * (nb - max_exact)).astype(np.int32)
    large = np.minimum(large, nb - 1)
    return (ret + np.where(n < max_exact, n, large)).astype(np.int32)


def neighborhood_attention(q, k, v, rpb):
    B, L, H, dh = q.shape
    R = L // GRID_W
    kr = min(NA_ROWS, R)
    ncb = GRID_W // NA_COLS
    qc = NA_COLS
    kw = 2 * NA_COLS
    qcol = np.arange(GRID_W).reshape(ncb, qc)
    cstart = np.clip(qcol - NA_COLS // 2, 0, GRID_W - NA_COLS)
    kstart = np.clip(np.arange(ncb) * qc - NA_COLS // 2, 0, GRID_W - kw)
    kcol = kstart[:, None] + np.arange(kw)
    col_mask = (kcol[:, None, :] >= cstart[:, :, None]) & (kcol[:, None, :] < cstart[:, :, None] + NA_COLS)
    dc_idx = np.clip(kcol[:, None, :] - qcol[:, :, None], -(NA_COLS - 1), NA_COLS - 1) + NA_COLS - 1
    scale = HEAD_DIM ** -0.5
    qg = (q * scale).reshape(B, R, ncb, qc, H, dh)
    kg = k.reshape(B, R, GRID_W, H, dh)[:, :, kcol]
    vg = v.reshape(B, R, GRID_W, H, dh)[:, :, kcol]
    mask = jnp.asarray(col_mask)[:, :, None, :]
    dc_j = jnp.asarray(dc_idx)[:, :, None, :]

    def row_step(args):
        q_r, r = args
        rs = jnp.clip(r - kr // 2, 0, R - kr)
        k_r = lax.dynamic_slice_in_dim(kg, rs, kr, axis=1)
        v_r = lax.dynamic_slice_in_dim(vg, rs, kr, axis=1)
        dr_idx = (rs + jnp.arange(kr) - r + NA_ROWS - 1)[None, None, :, None]
        bias = rpb[:, dr_idx, dc_j].astype(jnp.float32)
        s = jnp.einsum('bnqhd,bmnkhd->bhnqmk', q_r, k_r).astype(jnp.float32) + bias
        s = jnp.where(mask, s, NEG_INF)
        p = jax.nn.softmax(s.reshape(B, H, ncb, qc, kr * kw), axis=-1).reshape(s.shape).astype(v.dtype)
        return jnp.einsum('bhnqmk,bmnkhd->bnqhd', p, v_r)

    out = lax.map(row_step, (jnp.moveaxis(qg, 1, 0), jnp.arange(R)))
    return jnp.moveaxis(out, 0, 1).reshape(B, L, H * dh)


def window_attention(q, k, v, sink, t5_table):
    B, L, H, dh = q.shape
    G = H // SW_KV_HEADS
    nblk = L // SW_BLOCK
    qb = (q * HEAD_DIM ** -0.5).reshape(B, nblk, SW_BLOCK, SW_KV_HEADS, G, dh)
    pad = ((0, 0), (SW_BLOCK, SW_BLOCK), (0, 0), (0, 0))
    kp = jnp.pad(k, pad)
    vp = jnp.pad(v, pad)
    rel = np.arange(3 * SW_BLOCK)[None, :] - SW_BLOCK - np.arange(SW_BLOCK)[:, None]
    band = jnp.asarray(np.abs(rel) <= SW_WINDOW)
    bias = jnp.transpose(t5_table[jnp.asarray(t5_buckets(rel))], (2, 0, 1)).astype(jnp.float32)
    bias = bias.reshape(SW_KV_HEADS, G, SW_BLOCK, 3 * SW_BLOCK)
    sk = sink.astype(jnp.float32).reshape(SW_KV_HEADS, G)[None, :, :, None, None]

    def block_step(args):
        q_i, i = args
        k_i = lax.dynamic_slice_in_dim(kp, i * SW_BLOCK, 3 * SW_BLOCK, axis=1)
        v_i = lax.dynamic_slice_in_dim(vp, i * SW_BLOCK, 3 * SW_BLOCK, axis=1)
        kpos = i * SW_BLOCK - SW_BLOCK + jnp.arange(3 * SW_BLOCK)
        valid = band & ((kpos >= 0) & (kpos < L))[None, :]
        s = jnp.einsum('bqkgd,bmkd->bkgqm', q_i, k_i).astype(jnp.float32) + bias
        s = jnp.where(valid, s, NEG_INF)
        mx = jnp.maximum(jnp.max(s, axis=-1, keepdims=True), sk)
        e = jnp.exp(s - mx)
        p = (e / (jnp.sum(e, axis=-1, keepdims=True) + jnp.exp(sk - mx))).astype(v.dtype)
        return jnp.einsum('bkgqm,bmkd->bqkgd', p, v_i)

    out = lax.map(block_step, (jnp.moveaxis(qb, 1, 0), jnp.arange(nblk)))
    return jnp.moveaxis(out, 0, 1).reshape(B, L, H * dh)


def memory_attention(q, mem_h, w_kv):
    B, L = q.shape[0], q.shape[1]
    M = mem_h.shape[1]
    k, v = jnp.split(mem_h @ w_kv, 2, axis=-1)
    k = k.reshape(B, M, MEM_HEADS, MEM_HEAD_DIM)
    v = v.reshape(B, M, MEM_HEADS, MEM_HEAD_DIM)
    s = jnp.einsum('blhd,bmhd->bhlm', q * MEM_HEAD_DIM ** -0.5, k).astype(jnp.float32)
    p = jax.nn.softmax(s, axis=-1).astype(v.dtype)
    return jnp.einsum('bhlm,bmhd->blhd', p, v).reshape(B, L, MEM_WIDTH)


def trunk(x, mem, pre_norm, post_norm, mem_norm, w_in, w_mem_kv, w_branch_a, w_branch_b,
          w_branch_m, w_out, na_rpb, attn_sink, t5_bias):
    B, L, _ = x.shape
    split_points = [int(c) for c in np.cumsum(IN_SPLITS)[:-1]]
    for l in range(DEPTH):
        h = rms_norm(x, pre_norm[l])
        (qa, ka, va, za, qb, kb, vb, zb, qm, zm, ga, gb, gm) = jnp.split(h @ w_in[l], split_points, axis=-1)
        a = neighborhood_attention(qa.reshape(B, L, NA_HEADS, HEAD_DIM), ka.reshape(B, L, NA_HEADS, HEAD_DIM),
                                   va.reshape(B, L, NA_HEADS, HEAD_DIM), na_rpb[l]) * jax.nn.silu(za)
        b = window_attention(qb.reshape(B, L, SW_HEADS, HEAD_DIM), kb.reshape(B, L, SW_KV_HEADS, HEAD_DIM),
                             vb.reshape(B, L, SW_KV_HEADS, HEAD_DIM), attn_sink[l], t5_bias) * jax.nn.silu(zb)
        m = memory_attention(qm.reshape(B, L, MEM_HEADS, MEM_HEAD_DIM), rms_norm(mem, mem_norm[l]),
                             w_mem_kv[l]) * jax.nn.silu(zm)
        merged = (jax.nn.sigmoid(ga) * (a @ w_branch_a[l])
                  + jax.nn.sigmoid(gb) * (b @ w_branch_b[l])
                  + jax.nn.sigmoid(gm) * (m @ w_branch_m[l]))
        x = x + rms_norm(merged @ w_out[l], post_norm[l])
    return x


def setup_inputs(seed: int = 0) -> dict:
    key = jax.random.key(seed)
    ks = jax.random.split(key, 16)

    def nrm(k, shape, s):
        return jax.random.normal(k, shape, jnp.float32) * s

    return {
        'x_prompt': nrm(ks[0], (BATCH, SEQ, D_MODEL), 1.0),
        'x_sample': nrm(ks[1], (DEC_BATCH, DEC_SEQ, D_MODEL), 1.0),
        'mem_prompt': nrm(ks[2], (BATCH, MEM_TOKENS, D_MODEL), 1.0),
        'mem_sample': nrm(ks[3], (DEC_BATCH, MEM_TOKENS, D_MODEL), 1.0),
        'pre_norm': 1.0 + nrm(ks[4], (DEPTH, D_MODEL), 0.05),
        'post_norm': 1.0 + nrm(ks[5], (DEPTH, D_MODEL), 0.05),
        'mem_norm': 1.0 + nrm(ks[6], (DEPTH, D_MODEL), 0.05),
        'w_in': nrm(ks[7], (DEPTH, D_MODEL, IN_WIDTH), D_MODEL ** -0.5),
        'w_mem_kv': nrm(ks[8], (DEPTH, D_MODEL, 2 * MEM_WIDTH), D_MODEL ** -0.5),
        'w_branch_a': nrm(ks[9], (DEPTH, NA_WIDTH, D_MODEL), NA_WIDTH ** -0.5),
        'w_branch_b': nrm(ks[10], (DEPTH, SW_WIDTH, D_MODEL), SW_WIDTH ** -0.5),
        'w_branch_m': nrm(ks[11], (DEPTH, MEM_WIDTH, D_MODEL), MEM_WIDTH ** -0.5),
        'w_out': nrm(ks[12], (DEPTH, D_MODEL, D_MODEL), D_MODEL ** -0.5),
        'na_rpb': nrm(ks[13], (DEPTH, NA_HEADS, 2 * NA_ROWS - 1, 2 * NA_COLS - 1), 0.1),
        'attn_sink': nrm(ks[14], (DEPTH, SW_HEADS), 0.5),
        't5_bias': nrm(ks[15], (T5_BUCKETS, SW_HEADS), 0.1),
    }


def reference(x_prompt, x_sample, mem_prompt, mem_sample, pre_norm, post_norm, mem_norm, w_in, w_mem_kv,
              w_branch_a, w_branch_b, w_branch_m, w_out, na_rpb, attn_sink, t5_bias):
    y_prompt = trunk(x_prompt, mem_prompt, pre_norm, post_norm, mem_norm, w_in, w_mem_kv, w_branch_a,
                     w_branch_b, w_branch_m, w_out, na_rpb, attn_sink, t5_bias)
    y_sample = trunk(x_sample, mem_sample, pre_norm, post_norm, mem_norm, w_in, w_mem_kv, w_branch_a,
                     w_branch_b, w_branch_m, w_out, na_rpb, attn_sink, t5_bias)
    return (y_prompt, y_sample)
```

```python
import os
import numpy as np
import concourse.bass as bass
import concourse.mybir as mybir
from concourse.bass_utils import run_bass_kernel_spmd

F32 = mybir.dt.float32
BF16 = mybir.dt.bfloat16
AF = mybir.ActivationFunctionType
ALU = mybir.AluOpType
NEG = -30000.0
EPS = 1e-6


class Cfg:
    def __init__(self, D, NS, NCORES, L=2048):
        self.D = D; self.KC = D // 128; self.NS = NS; self.NCORES = NCORES; self.L = L
        self.NT = L // 128; self.R = L // 64
        self.INW = 12288 + 3 * D; self.NCH = self.INW // 128
        self.GW = 256; self.NG = self.INW // 256
        self.OGW = 256; self.NOG = D // 256
        self.TB = 512; self.NB = L // 512
        self.TB2 = 256; self.NB2 = L // 256


def t5_buckets(rel):
    nb = 16; max_exact = 8
    ret = (rel > 0).astype(np.int32) * nb
    n = np.abs(rel)
    large = max_exact + (np.log(np.maximum(n, 1) / max_exact) / np.log(128 / max_exact) * (nb - max_exact)).astype(np.int32)
    large = np.minimum(large, nb - 1)
    return (ret + np.where(n < max_exact, n, large)).astype(np.int32)


def host_consts():
    ident = np.eye(128, dtype=np.float32)
    rel = np.arange(511) - 255
    bk = t5_buckets(rel)
    valid = np.abs(rel) <= 128
    oh = np.zeros((33, 511), np.float32)
    for r in range(511):
        if valid[r]:
            oh[bk[r], r] = 1.0
        else:
            oh[32, r] = NEG
    qc = np.arange(64)
    cstart = np.clip(qc - 8, 0, 48)
    kc = np.arange(64)
    cm = np.where((kc[None, :] >= cstart[:, None]) & (kc[None, :] < cstart[:, None] + 16), 0.0, NEG).astype(np.float32)
    cm128 = np.concatenate([cm, cm], axis=0)
    cm16 = np.ascontiguousarray(np.broadcast_to(cm128[:, None, :], (128, 16, 64))).reshape(128, 1024)
    return ident, oh, cm16


class Op:
    __slots__ = ("eng", "fn", "deps", "dma", "sig", "cnt", "sem", "target", "prev_target")


class Prog:
    CE = ("pe", "act", "dve", "pool")

    def __init__(self):
        self.ops = []
        self.last_w = {}
        self.rd_eng = {}
        self.rd_dma = {}
        self.last_on = {}
        self.dmas_since_bar = []
        self.pending_bar = {}

    def add(self, eng, fn, reads=(), writes=(), dma=False):
        op = Op(); op.eng = eng; op.fn = fn; op.dma = dma; op.sig = False
        idx = len(self.ops)
        deps = set()
        for k in reads:
            w = self.last_w.get(k)
            if w is not None:
                deps.add(w)
        for k in writes:
            w = self.last_w.get(k)
            if w is not None:
                deps.add(w)
            for v in self.rd_eng.get(k, {}).values():
                deps.add(v)
            for v in self.rd_dma.get(k, ()):
                deps.add(v)
        pb = self.pending_bar.pop(eng, None)
        if pb:
            deps |= pb
        deps.discard(idx)
        op.deps = deps
        for k in reads:
            if dma:
                self.rd_dma.setdefault(k, []).append(idx)
            else:
                self.rd_eng.setdefault(k, {})[eng] = idx
        for k in writes:
            self.last_w[k] = idx
            self.rd_eng[k] = {}
            self.rd_dma[k] = []
        self.ops.append(op)
        if dma:
            self.dmas_since_bar.append(idx)
        else:
            self.last_on[eng] = idx
        return idx

    def barrier(self):
        s = set(self.last_on.values()) | set(self.dmas_since_bar)
        for e in ("pe", "act", "dve", "pool", "sp"):
            self.pending_bar[e] = set(s) | self.pending_bar.get(e, set())
        self.dmas_since_bar = []

    def finalize(self, nsem_dma=24):
        ops = self.ops
        for op in ops:
            for d in op.deps:
                ops[d].sig = True
        cnt = {e: 0 for e in self.CE}
        ndma = 0
        for op in ops:
            if op.dma:
                op.sem = ndma % nsem_dma
                op.target = 16 * (ndma // nsem_dma + 1)
                ndma += 1
            elif op.sig:
                cnt[op.eng] += 1
                op.cnt = cnt[op.eng]
        self.nsem_dma = nsem_dma
        self.ndma = ndma

    def emit(self, eng_name, eng, csem, dsem):
        ops = self.ops
        seen = {}
        for op in ops:
            if op.eng != eng_name:
                continue
            need = {}
            for d in op.deps:
                o = ops[d]
                if o.dma:
                    key = ("d", o.sem); val = o.target
                else:
                    if o.eng == eng_name and eng_name == "pe":
                        continue
                    key = ("c", o.eng); val = o.cnt
                if need.get(key, 0) < val:
                    need[key] = val
            if op.dma and op.target > 16:
                key = ("d", op.sem)
                if need.get(key, 0) < op.target - 16:
                    need[key] = op.target - 16
            for key, val in need.items():
                if seen.get(key, 0) >= val:
                    continue
                seen[key] = val
                sem = dsem[key[1]] if key[0] == "d" else csem[key[1]]
                eng.wait_ge(sem, val)
            ins = op.fn(eng)
            if op.dma:
                ins.then_inc(dsem[op.sem], 16)
            elif op.sig:
                ins.then_inc(csem[op.eng], 1)
        if eng_name == "sp":
            for s in range(min(self.nsem_dma, self.ndma)):
                n = (self.ndma - 1 - s) // self.nsem_dma + 1
                if seen.get(("d", s), 0) < 16 * n:
                    eng.wait_ge(dsem[s], 16 * n)


def chunk_kind(c):
    if c < 12: return "qs"
    if c < 36: return "cp"
    if c < 48: return "silu"
    if c < 60: return "qs"
    if c < 68: return "cp"
    if c < 80: return "silu"
    if c < 88: return "qm"
    if c < 96: return "silu"
    return "sig"


def build(cfg):
    D, KC, NS, L, NT, R = cfg.D, cfg.KC, cfg.NS, cfg.L, cfg.NT, cfg.R
    INW, NCH, NG, NOG = cfg.INW, cfg.NCH, cfg.NG, cfg.NOG
    nc = bass.Bass("TRN2", target_bir_lowering=False)

    def din(name, shape, dt=F32):
        return nc.dram_tensor(name, list(shape), dt, kind="ExternalInput")

    x_in = din("x", [NS, L, D]); mem_in = din("mem", [NS, 256, D])
    gpre_in = din("gpre", [2, 128, KC]); gmem_in = din("gmem", [2, 128, KC]); gpost_in = din("gpost", [2, 128, D])
    w_in = din("w_in", [2, D, INW]); w_kv = din("w_kv", [2, D, 2048])
    w_a = din("w_a", [2, 1536, D]); w_b = din("w_b", [2, 1536, D]); w_m = din("w_m", [2, 1024, D]); w_o = din("w_o", [2, D, D])
    rpb_in = din("rpb", [2, 12, 128, 465]); sink_in = din("sink", [2, 128, 12]); t5_in = din("t5", [32, 12])
    ident_in = din("ident", [128, 128]); oh_in = din("oh", [33, 511]); cm_in = din("cm16", [128, 1024])
    y_out = nc.dram_tensor("y", [NS, L, D], F32, kind="ExternalOutput")

    WINL = [nc.dram_tensor(f"s_win{i}", [NG, 128, KC, 256], BF16) for i in range(2)]
    WKV = nc.dram_tensor("s_wkv", [2, 8, 128, KC, 256], BF16)
    WBR = nc.dram_tensor("s_wbr", [2, KC, 128, 32, 128], BF16)
    WOUT = nc.dram_tensor("s_wout", [2, NOG, 128, KC, 256], BF16)
    dbgk = {"kind": "ExternalOutput"} if os.environ.get("K_DBG") else {}
    X1 = nc.dram_tensor("s_x1", [NS, L, D], F32, **dbgk)
    PROJ = nc.dram_tensor("s_proj", [NCH, 128, L], BF16, **dbgk)
    ATT = nc.dram_tensor("s_att", [32, 128, L], BF16, **dbgk)
    FVS = nc.dram_tensor("s_fvs", [12, 128, 511], F32)
    RPD = nc.dram_tensor("s_rpd", [12, 15, 64, 127], F32)

    P = Prog()
    STOP = os.environ.get('K_STOP', '')
    stopped = [False]
    ARENA_ELEMS = 103 * 1024

    from contextlib import ExitStack
    with ExitStack() as es:
        arena_t = es.enter_context(nc.sbuf_tensor("arena", [128, ARENA_ELEMS], BF16))
        banks = [es.enter_context(nc.psum_tensor(f"pb{i}", [128, 512], F32)) for i in range(6)]
        ptb = [es.enter_context(nc.psum_tensor(f"pt{i}", [128, 1024], BF16)) for i in range(2)]
        csem = {e: es.enter_context(nc.semaphore(f"c_{e}")) for e in Prog.CE}
        NSEMD = 24
        dsem = [es.enter_context(nc.semaphore(f"d_{i}")) for i in range(NSEMD)]

        def V(off, dt, shape, parts=128, p0=0):
            n = int(np.prod(shape)); esz = 2 if dt is BF16 else 4
            assert off % 4 == 0
            assert off + n * esz <= ARENA_ELEMS * 2, (off, n, esz)
            a = arena_t[p0:p0 + parts, off // 2: off // 2 + n * esz // 2]
            if dt is not BF16:
                a = a.bitcast(dt)
            if len(shape) == 2:
                a = a.rearrange("p (a b) -> p a b", a=shape[0])
            elif len(shape) == 3:
                a = a.rearrange("p (a b c) -> p a b c", a=shape[0], b=shape[1])
            return a

        class Alloc:
            def __init__(self, base): self.o = base
            def get(self, nbytes):
                o = self.o; self.o += (nbytes + 31) // 32 * 32; return o

        pa = Alloc(0)
        o_ident = pa.get(256); o_ones = pa.get(256); o_ET = pa.get(12 * 3 * 128 * 2); o_TBq = pa.get(12 * 16 * 64 * 2)
        o_gpre = pa.get(KC * 4); o_gmem = pa.get(KC * 4); o_gpost = pa.get(D * 4); o_esink = pa.get(48)
        o_small = pa.get(64 * 4); o_KmT = pa.get(8 * 256 * 2); o_Vm = pa.get(2 * 1024 * 2)
        PBASE = pa.o
        ident_b = V(o_ident, BF16, [128]); ones_b = V(o_ones, BF16, [128])
        ET = V(o_ET, BF16, [12, 3, 128]); TBq = V(o_TBq, BF16, [12, 16, 64])
        gpre = V(o_gpre, F32, [KC]); gmem = V(o_gmem, F32, [KC]); gpost = V(o_gpost, F32, [D]); esink = V(o_esink, F32, [12])
        small = V(o_small, F32, [64])
        KmT = V(o_KmT, BF16, [8, 256]); Vm = V(o_Vm, BF16, [2, 1024])

        def dma(out, in_, reads=(), writes=()):
            P.add("sp", lambda e, out=out, in_=in_: e.dma_start(out=out, in_=in_), reads, writes, dma=True)

        a0 = Alloc(PBASE)
        o_idf = a0.get(512); o_t33 = a0.get(48); o_oh = a0.get(511 * 4); o_l33 = a0.get(512); o_on33 = a0.get(512)
        o_w511 = a0.get(2 * 511 * 4); o_et32 = a0.get(2 * 384 * 4)
        idf = V(o_idf, F32, [128]); t33 = V(o_t33, F32, [12], parts=33); ohs = V(o_oh, F32, [511], parts=33)
        l33 = V(o_l33, F32, [128], parts=33); on33 = V(o_on33, F32, [128], parts=33)
        dma(idf, ident_in.ap(), writes=["idf"])
        P.add("dve", lambda e: e.tensor_copy(out=ident_b, in_=idf), ["idf"], ["ident"])
        P.add("dve", lambda e: e.memset(ones_b, 1.0), [], ["ones"])
        P.add("dve", lambda e: e.memset(t33, 1.0), [], ["t33"])
        P.add("dve", lambda e: e.memset(on33, 1.0), [], ["on33"])
        dma(V(o_t33, F32, [12], parts=32), t5_in.ap(), writes=["t33"])
        dma(ohs, oh_in.ap(), writes=["oh"])
        for h in range(12):
            w511 = V(o_w511 + (h % 2) * 511 * 4, F32, [511]); et32 = V(o_et32 + (h % 2) * 384 * 4, F32, [3, 128])
            kw, ke = ("w511", h % 2), ("et32", h % 2)
            P.add("dve", lambda e, h=h: e.tensor_scalar(out=l33, in0=on33, scalar1=t33[:, h:h + 1], scalar2=None, op0=ALU.mult), ["on33", "t33"], ["l33"])
            P.add("pe", lambda e: e.matmul(banks[0][:, 0:511], lhsT=l33, rhs=ohs, start=True, stop=True), ["l33", "oh"], ["b0"])
            P.add("dve", lambda e, w511=w511: e.tensor_copy(out=w511, in_=banks[0][:, 0:511]), ["b0"], [kw])
            dma(FVS.ap()[h], w511, [kw], [("fvs", h)])
            src = bass.AP(FVS, h * 128 * 511 + 127, [[510, 128], [128, 3], [1, 128]])
            dma(et32, src, [("fvs", h)], [ke])
            P.add("dve", lambda e, h=h, et32=et32: e.tensor_copy(out=ET[:, h, :, :], in_=et32), [ke], ["ET"])
        P.barrier()
        if STOP == 'setup': stopped[0] = True

        cvt_jobs = []

        def add_cvt(src2d, K, N, gw, dst_fn):
            CW = min(N, 2048)
            for kc in range(K // 128):
                for c0 in range(0, N, CW):
                    cw = min(CW, N - c0)
                    cvt_jobs.append((src2d[kc * 128:(kc + 1) * 128, c0:c0 + cw], cw, gw, dst_fn(kc, c0 // gw, cw // gw)))

        for l in range(2):
            add_cvt(w_in.ap()[l], D, INW, 256, lambda kc, g0, ng, l=l: WINL[l].ap()[g0:g0 + ng, :, kc, :].rearrange("g p w -> p g w"))
            add_cvt(w_kv.ap()[l], D, 2048, 256, lambda kc, g0, ng, l=l: WKV.ap()[l, g0:g0 + ng, :, kc, :].rearrange("g p w -> p g w"))
            for wsrc, kk, koff in ((w_a, 1536, 0), (w_b, 1536, 12), (w_m, 1024, 24)):
                add_cvt(wsrc.ap()[l], kk, D, 128, lambda kc, g0, ng, l=l, koff=koff: WBR.ap()[l, g0:g0 + ng, :, koff + kc, :].rearrange("g p w -> p g w"))
            add_cvt(w_o.ap()[l], D, D, 256, lambda kc, g0, ng, l=l: WOUT.ap()[l, g0:g0 + ng, :, kc, :].rearrange("g p w -> p g w"))
        a0 = Alloc(PBASE)
        NCB = 6
        o_cin = [a0.get(2048 * 4) for _ in range(NCB)]; o_cout = [a0.get(2048 * 2) for _ in range(NCB)]

        def cvt_load(i):
            src, CW, gw, dst = cvt_jobs[i]
            dma(V(o_cin[i % NCB], F32, [CW]), src, writes=[("cin", i % NCB)])
        if stopped[0]: cvt_jobs = []
        for i in range(min(NCB, len(cvt_jobs))):
            cvt_load(i)
        ceng = ("dve", "act", "pool")
        for i, (src, CW, gw, dst) in enumerate(cvt_jobs):
            cin = V(o_cin[i % NCB], F32, [CW]); cout = V(o_cout[i % NCB], BF16, [CW])
            en = ceng[i % 3]
            if en == "act":
                P.add("act", lambda e, cin=cin, cout=cout: e.copy(out=cout, in_=cin), [("cin", i % NCB)], [("cout", i % NCB)])
            else:
                P.add(en, lambda e, cin=cin, cout=cout: e.tensor_copy(out=cout, in_=cin), [("cin", i % NCB)], [("cout", i % NCB)])
            dma(dst, V(o_cout[i % NCB], BF16, [CW // gw, gw]), [("cout", i % NCB)], [])
            if i + NCB < len(cvt_jobs):
                cvt_load(i + NCB)
        P.barrier()
        if STOP == 'cvt': stopped[0] = True

        def rms_tile(src_dram, xt, junk, hb, ssc, tag):
            ss, rs = ssc
            dma(xt, src_dram, writes=[("xt", tag)])
            P.add("dve", lambda e: e.memset(ss, 0.0), [], [("ss", tag)])
            P.add("act", lambda e: e.activation(out=junk, in_=xt, func=AF.Square, accum_out=ss), [("xt", tag), ("ss", tag)], [("ss", tag), ("hb", tag)])
            P.add("dve", lambda e: e.tensor_scalar(out=rs, in0=ss, scalar1=1.0 / D, scalar2=EPS, op0=ALU.mult, op1=ALU.add), [("ss", tag)], [("rs", tag)])
            P.add("act", lambda e: e.activation(out=rs, in_=rs, func=AF.Sqrt), [("rs", tag)], [("rs", tag)])
            P.add("dve", lambda e: e.reciprocal(out=rs, in_=rs), [("rs", tag)], [("rs", tag)])
            P.add("dve", lambda e: e.tensor_scalar(out=hb, in0=xt, scalar1=rs, scalar2=None, op0=ALU.mult), [("xt", tag), ("rs", tag)], [("hb", tag)])

        tcount = [0]

        def transpose_to(hb, tag, gain, dstT, col0):
            for k0 in range(0, KC, 8):
                nk = min(8, KC - k0)
                pt = ptb[tcount[0] % 2]; pk = ("pt", tcount[0] % 2); tcount[0] += 1
                for j in range(nk):
                    kc = k0 + j
                    P.add("pe", lambda e, pt=pt, j=j, kc=kc: e.transpose(out=pt[:, j * 128:(j + 1) * 128], in_=hb[:, kc * 128:(kc + 1) * 128], identity=ident_b),
                          [("hb", tag), "ident"], [pk])
                for j in range(nk):
                    kc = k0 + j
                    if j % 2 == 0:
                        P.add("act", lambda e, pt=pt, j=j, kc=kc: e.activation(out=dstT[:, kc, col0:col0 + 128], in_=pt[:, j * 128:(j + 1) * 128], func=AF.Copy, scale=gain[:, kc:kc + 1]),
                              [pk], [("hT", col0)])
                    else:
                        P.add("dve", lambda e, pt=pt, j=j, kc=kc: e.tensor_scalar(out=dstT[:, kc, col0:col0 + 128], in0=pt[:, j * 128:(j + 1) * 128], scalar1=gain[:, kc:kc + 1], scalar2=None, op0=ALU.mult),
                              [pk], [("hT", col0)])

        aA = Alloc(PBASE)
        o_xt = [aA.get(D * 4) for _ in range(2)]; o_hb = [aA.get(D * 2) for _ in range(2)]
        o_hT = aA.get(KC * 512 * 2); o_wb = [aA.get(KC * 256 * 2) for _ in range(3)]; o_stg = [aA.get(4 * 512 * 2) for _ in range(2)]
        assert aA.o <= ARENA_ELEMS * 2, aA.o
        wcount = [0]

        def wload(src):
            i = wcount[0] % 3; wcount[0] += 1
            wb = V(o_wb[i], BF16, [KC, 256])
            dma(wb, src, writes=[("wb", i)])
            return wb, ("wb", i)

        bcount = [0]

        def nextbank(n=6):
            i = bcount[0] % n; bcount[0] += 1
            return banks[i], ("bank", i)

        for l in range(2):
            if stopped[0]: break
            dma(gpre, gpre_in.ap()[l], writes=["gpre"]); dma(gmem, gmem_in.ap()[l], writes=["gmem"])
            dma(gpost, gpost_in.ap()[l], writes=["gpost"]); dma(esink, sink_in.ap()[l], writes=["esink"])
            P.add("act", lambda e: e.activation(out=esink, in_=esink, func=AF.Exp), ["esink"], ["esink"])
            aT = Alloc(PBASE)
            o_rpb = aT.get(15 * 127 * 4); o_tb32 = aT.get(16 * 64 * 4); o_cm = aT.get(1024 * 4)
            rpbt = V(o_rpb, F32, [15, 127]); tb32 = V(o_tb32, F32, [16, 64]); cmt = V(o_cm, F32, [16, 64])
            dma(cmt, cm_in.ap().rearrange("p (a b) -> p a b", a=16), writes=["cm"])
            P.add("dve", lambda e: e.memset(rpbt, NEG), [], ["rpbt"])
            for h in range(12):
                dma(rpbt[:, :, 48:79], rpb_in.ap()[l, h].rearrange("p (a b) -> p a b", a=15), writes=["rpbt"])
                dma(RPD.ap()[h].rearrange("d r w -> r d w"), V(o_rpb, F32, [15, 127], parts=64), ["rpbt"], [("rpd", h)])
                P.add("dve", lambda e: e.memset(tb32, NEG), [], ["tb32"])
                base = h * 15 * 64 * 127
                dma(V(o_tb32, F32, [16, 64], parts=64)[:, 1:16, :], bass.AP(RPD, base + 63, [[126, 64], [64 * 127, 15], [1, 64]]), [("rpd", h)], ["tb32"])
                dma(V(o_tb32, F32, [16, 64], parts=64, p0=64)[:, 2:16, :], bass.AP(RPD, base + 63, [[126, 64], [64 * 127, 14], [1, 64]]), [("rpd", h)], ["tb32"])
                P.add("dve", lambda e, h=h: e.tensor_tensor(out=TBq[:, h, :, :], in0=tb32, in1=cmt, op=ALU.add), ["tb32", "cm"], ["TBq"])
            P.barrier()
            if STOP == 'tables': stopped[0] = True; break

            for s in range(NS):
                memhT = V(o_hT, BF16, [KC, 256])
                for i in range(2):
                    xt = V(o_xt[i], F32, [D]); hb = V(o_hb[i], BF16, [D])
                    rms_tile(mem_in.ap()[s, i * 128:(i + 1) * 128, :], xt, hb, hb, (small[:, i:i + 1], small[:, 8 + i:9 + i]), i)
                    transpose_to(hb, i, gmem, memhT, i * 128)
                hkeys = [("hT", 0), ("hT", 128)]
                for g in range(8):
                    wb, wk = wload(WKV.ap()[l, g])
                    if g < 4:
                        for j in range(2):
                            bk, bkk = nextbank()
                            for kc in range(KC):
                                P.add("pe", lambda e, bk=bk, wb=wb, j=j, kc=kc: e.matmul(bk[:, 0:256], lhsT=wb[:, kc, j * 128:(j + 1) * 128], rhs=memhT[:, kc, :], start=(kc == 0), stop=(kc == KC - 1)),
                                      [wk] + hkeys, [bkk])
                            P.add("dve", lambda e, bk=bk, g=g, j=j: e.tensor_copy(out=KmT[:, 2 * g + j, :], in_=bk[:, 0:256]), [bkk], ["KmT"])
                    else:
                        for mt in range(2):
                            bk, bkk = nextbank()
                            for kc in range(KC):
                                P.add("pe", lambda e, bk=bk, wb=wb, mt=mt, kc=kc: e.matmul(bk[:, 0:256], lhsT=memhT[:, kc, mt * 128:(mt + 1) * 128], rhs=wb[:, kc, :], start=(kc == 0), stop=(kc == KC - 1)),
                                      [wk] + hkeys, [bkk])
                            P.add("dve", lambda e, bk=bk, g=g, mt=mt: e.tensor_copy(out=Vm[:, mt, (g - 4) * 256:(g - 3) * 256], in_=bk[:, 0:256]), [bkk], ["Vm"])
                P.barrier()

                if STOP == 'M': stopped[0] = True; break
                xsrc = x_in if l == 0 else X1
                for b in range(cfg.NB):
                    hT = V(o_hT, BF16, [KC, 512])
                    for i in range(4):
                        xt = V(o_xt[i % 2], F32, [D]); hb = V(o_hb[i % 2], BF16, [D])
                        r0 = b * 512 + i * 128
                        rms_tile(xsrc.ap()[s, r0:r0 + 128, :], xt, hb, hb, (small[:, i:i + 1], small[:, 8 + i:9 + i]), i % 2)
                        transpose_to(hb, i % 2, gpre, hT, i * 128)
                    hkeys = [("hT", i * 128) for i in range(4)]
                    pend = [wload(WINL[l].ap()[g]) for g in range(min(2, NG))]
                    for g in range(NG):
                        wb, wk = pend.pop(0)
                        for j in range(2):
                            c = 2 * g + j
                            bk, bkk = nextbank()
                            for kc in range(KC):
                                P.add("pe", lambda e, bk=bk, wb=wb, j=j, kc=kc: e.matmul(bk[:, 0:512], lhsT=wb[:, kc, j * 128:(j + 1) * 128], rhs=hT[:, kc, :], start=(kc == 0), stop=(kc == KC - 1)),
                                      [wk] + hkeys, [bkk])
                            si = (c // 4) % 2
                            stg = V(o_stg[si], BF16, [4, 512])
                            dst = stg[:, c % 4, :]
                            kind = chunk_kind(c)
                            if kind == "cp":
                                P.add("dve", lambda e, bk=bk, dst=dst: e.tensor_copy(out=dst, in_=bk[:, 0:512]), [bkk], [("stg", si)])
                            elif kind == "qs":
                                P.add("dve", lambda e, bk=bk, dst=dst: e.tensor_scalar(out=dst, in0=bk[:, 0:512], scalar1=128.0 ** -0.5, scalar2=None, op0=ALU.mult), [bkk], [("stg", si)])
                            elif kind == "qm":
                                P.add("dve", lambda e, bk=bk, dst=dst: e.tensor_scalar(out=dst, in0=bk[:, 0:512], scalar1=1.0 / 16.0, scalar2=None, op0=ALU.mult), [bkk], [("stg", si)])
                            elif kind == "silu":
                                P.add("act", lambda e, bk=bk, dst=dst: e.activation(out=dst, in_=bk[:, 0:512], func=AF.Silu), [bkk], [("stg", si)])
                            else:
                                P.add("act", lambda e, bk=bk, dst=dst: e.activation(out=dst, in_=bk[:, 0:512], func=AF.Sigmoid), [bkk], [("stg", si)])
                            if c % 4 == 3 or c == NCH - 1:
                                cb = c - (c % 4)
                                dma(PROJ.ap()[cb:c + 1, :, b * 512:(b + 1) * 512].rearrange("c p t -> p c t"), stg[:, 0:c - cb + 1, :], [("stg", si)], [])
                        if g + 2 < NG:
                            pend.append(wload(WINL[l].ap()[g + 2]))
                P.barrier()

                if STOP == 'A': stopped[0] = True; break
                aN = Alloc(PBASE)
                o_q = [aN.get(L * 2) for _ in range(2)]; o_k = [aN.get(L * 2) for _ in range(2)]; o_v = [aN.get(L * 2) for _ in range(2)]
                o_z = [aN.get(L * 2) for _ in range(2)]; o_vt = [aN.get(L * 2) for _ in range(2)]; o_oh_ = [aN.get(L * 2) for _ in range(2)]
                o_pt = [aN.get(5 * 128 * 2) for _ in range(2)]; o_rd = [aN.get(128 * 4) for _ in range(2)]; o_of = [aN.get(128 * 4) for _ in range(2)]

                def vtrans(VT, vk, Vt, vtk):
                    for half in range(2):
                        pt = ptb[tcount[0] % 2]; pk = ("pt", tcount[0] % 2); tcount[0] += 1
                        for j in range(8):
                            u = half * 8 + j
                            P.add("pe", lambda e, pt=pt, j=j, u=u: e.transpose(out=pt[:, j * 128:(j + 1) * 128], in_=VT[:, u * 128:(u + 1) * 128], identity=ident_b), [vk, "ident"], [pk])
                        P.add("dve", lambda e, pt=pt, half=half: e.tensor_copy(out=Vt[:, half * 1024:(half + 1) * 1024], in_=pt[:, 0:1024]), [pk], [vtk])

                def rs_of(r):
                    return min(max(r - 4, 0), R - 8)
                for h in range(12):
                    p2 = h % 2
                    QT = V(o_q[p2], BF16, [L]); KT = V(o_k[p2], BF16, [L]); VT = V(o_v[p2], BF16, [L]); ZT = V(o_z[p2], BF16, [L])
                    Vt = V(o_vt[p2], BF16, [L]); OH = V(o_oh_[p2], BF16, [L])
                    dma(QT, PROJ.ap()[h], writes=[("q", p2)]); dma(KT, PROJ.ap()[12 + h], writes=[("k", p2)])
                    dma(VT, PROJ.ap()[24 + h], writes=[("v", p2)]); dma(ZT, PROJ.ap()[36 + h], writes=[("z", p2)])
                    vtrans(VT, ("v", p2), Vt, ("vt", p2))
                    def na_iter(t, part, h=h, p2=p2, QT=QT, KT=KT, ZT=ZT, Vt=Vt, OH=OH):
                        ul = []
                        for u in range(NT):
                            inv = []
                            for kr in range(2):
                                for qr in range(2):
                                    rs = rs_of(2 * t + qr)
                                    if not (rs <= 2 * u + kr <= rs + 7):
                                        inv.append((kr, qr))
                            if len(inv) < 4:
                                ul.append((u, inv))
                        assert 1 <= len(ul) <= 5
                        it = h * NT + t; p3 = it % 2
                        S0, s0k = banks[2 * p3], ("bank", 2 * p3)
                        S1, s1k = banks[2 * p3 + 1], ("bank", 2 * p3 + 1)
                        ACC, acck = banks[4 + p3], ("bank", 4 + p3)
                        Pt = V(o_pt[p3], BF16, [5, 128]); ptk = ("Pt", p3)
                        for idx, (u, inv) in enumerate(ul if part == "front" else []):
                            Sb, sk, col = (S0, s0k, idx * 128) if idx < 4 else (S1, s1k, 0)
                            e0 = 2 * (u - t) + 8
                            assert 0 <= e0 <= 14
                            P.add("pe", lambda e, Sb=Sb, col=col, u=u, t=t, KT=KT, QT=QT: e.matmul(Sb[:, col:col + 128], lhsT=KT[:, u * 128:(u + 1) * 128], rhs=QT[:, t * 128:(t + 1) * 128], start=True, stop=False),
                                  [("k", p2), ("q", p2)], [sk])
                            P.add("pe", lambda e, Sb=Sb, col=col, h=h, e0=e0: e.matmul(Sb[:, col:col + 128], lhsT=TBq[:, h, e0:e0 + 2, :], rhs=ident_b, start=False, stop=True),
                                  ["TBq", "ident"], [sk])
                        if part == "front":
                            return
                        n0 = min(4, len(ul))
                        P.add("act", lambda e, S0=S0, Pt=Pt, n0=n0: e.activation(out=Pt[:, 0:n0, :], in_=S0[:, 0:n0 * 128].rearrange("p (a b) -> p a b", a=n0), func=AF.Exp), [s0k], [ptk])
                        if len(ul) == 5:
                            P.add("act", lambda e, S1=S1, Pt=Pt: e.activation(out=Pt[:, 4, :], in_=S1[:, 0:128], func=AF.Exp), [s1k], [ptk])
                        for idx, (u, inv) in enumerate(ul):
                            for (kr, qr) in inv:
                                Pz = V(o_pt[p3], BF16, [5, 128], parts=64, p0=kr * 64)
                                P.add("dve", lambda e, Pz=Pz, idx=idx, qr=qr: e.memset(Pz[:, idx, qr * 64:(qr + 1) * 64], 0.0), [], [ptk])
                        nu = len(ul)
                        for idx, (u, inv) in enumerate(ul):
                            P.add("pe", lambda e, ACC=ACC, Vt=Vt, Pt=Pt, u=u, idx=idx, nu=nu: e.matmul(ACC[:, 0:128], lhsT=Vt[:, u * 128:(u + 1) * 128], rhs=Pt[:, idx, :], start=(idx == 0), stop=(idx == nu - 1)),
                                  [("vt", p2), ptk], [acck])
                        for idx, (u, inv) in enumerate(ul):
                            P.add("pe", lambda e, ACC=ACC, Pt=Pt, idx=idx, nu=nu: e.matmul(ACC[:, 128:256], lhsT=ones_b, rhs=Pt[:, idx, :], start=(idx == 0), stop=(idx == nu - 1)),
                                  ["ones", ptk], [acck])
                        rd = V(o_rd[p3], F32, [128]); of = V(o_of[p3], F32, [128])
                        P.add("dve", lambda e, ACC=ACC, rd=rd: e.reciprocal(out=rd, in_=ACC[:, 128:256]), [acck], [("rd", p3)])
                        P.add("dve", lambda e, ACC=ACC, rd=rd, of=of: e.tensor_tensor(out=of, in0=ACC[:, 0:128], in1=rd, op=ALU.mult), [acck, ("rd", p3)], [("of", p3)])
                        P.add("dve", lambda e, of=of, ZT=ZT, OH=OH, t=t: e.tensor_tensor(out=OH[:, t * 128:(t + 1) * 128], in0=of, in1=ZT[:, t * 128:(t + 1) * 128], op=ALU.mult),
                              [("of", p3), ("z", p2)], [("oh", p2)])
                    na_iter(0, "front")
                    for t in range(NT):
                        if t + 1 < NT:
                            na_iter(t + 1, "front")
                        na_iter(t, "back")
                    dma(ATT.ap()[h], OH, [("oh", p2)], [])
                P.barrier()

                if STOP == 'NA': stopped[0] = True; break
                aS = Alloc(PBASE)
                o_k = [aS.get(L * 2) for _ in range(2)]; o_v = [aS.get(L * 2) for _ in range(2)]; o_vt = [aS.get(L * 2) for _ in range(2)]
                o_q3 = [aS.get(3 * L * 2) for _ in range(2)]; o_z3 = [aS.get(3 * L * 2) for _ in range(2)]; o_o3 = [aS.get(3 * L * 2) for _ in range(2)]
                o_p3 = [aS.get(384 * 2) for _ in range(3)]; o_dt = [aS.get(384 * 4) for _ in range(2)]; o_of3 = [aS.get(384 * 4) for _ in range(2)]
                assert aS.o <= ARENA_ELEMS * 2
                for kvh in range(4):
                    p2 = kvh % 2
                    KT = V(o_k[p2], BF16, [L]); VT = V(o_v[p2], BF16, [L]); Vt = V(o_vt[p2], BF16, [L])
                    Q3 = V(o_q3[p2], BF16, [3, L]); Z3 = V(o_z3[p2], BF16, [3, L]); O3 = V(o_o3[p2], BF16, [3, L])
                    dma(KT, PROJ.ap()[60 + kvh], writes=[("k", p2)]); dma(VT, PROJ.ap()[64 + kvh], writes=[("v", p2)])
                    dma(Q3, PROJ.ap()[48 + 3 * kvh:51 + 3 * kvh].rearrange("c p t -> p c t"), writes=[("q", p2)])
                    dma(Z3, PROJ.ap()[68 + 3 * kvh:71 + 3 * kvh].rearrange("c p t -> p c t"), writes=[("z", p2)])
                    vtrans(VT, ("v", p2), Vt, ("vt", p2))
                    for t in range(NT):
                        ul = [u for u in (t - 1, t, t + 1) if 0 <= u < NT]
                        it = kvh * NT + t; p3 = it % 2
                        ACO, acok = banks[3], ("bank", 3)
                        DEN, denk = banks[4], ("bank", 4)
                        if p3:
                            ACO, acok = banks[5], ("bank", 5)
                        for u in ul:
                            j = u - t + 1
                            Sb, sk = banks[j], ("bank", j)
                            P.add("pe", lambda e, Sb=Sb, KT=KT, Q3=Q3, u=u, t=t: e.matmul(Sb[:, 0:384], lhsT=KT[:, u * 128:(u + 1) * 128], rhs=Q3[:, :, t * 128:(t + 1) * 128], start=True, stop=False),
                                  [("k", p2), ("q", p2)], [sk])
                            for hh in range(3):
                                P.add("pe", lambda e, Sb=Sb, hh=hh, kvh=kvh, j=j: e.matmul(Sb[:, hh * 128:(hh + 1) * 128], lhsT=ET[:, 3 * kvh + hh, j, :], rhs=ident_b, start=False, stop=(hh == 2)),
                                      ["ET", "ident"], [sk])
                            Pj = V(o_p3[j], BF16, [384])
                            P.add("act", lambda e, Sb=Sb, Pj=Pj: e.activation(out=Pj, in_=Sb[:, 0:384], func=AF.Exp), [sk], [("P3", j)])
                        for i2, u in enumerate(ul):
                            j = u - t + 1; Pj = V(o_p3[j], BF16, [384])
                            P.add("pe", lambda e, ACO=ACO, Vt=Vt, Pj=Pj, u=u, i2=i2, n=len(ul): e.matmul(ACO[:, 0:384], lhsT=Vt[:, u * 128:(u + 1) * 128], rhs=Pj, start=(i2 == 0), stop=(i2 == n - 1)),
                                  [("vt", p2), ("P3", j)], [acok])
                        for i2, u in enumerate(ul):
                            j = u - t + 1; Pj = V(o_p3[j], BF16, [384])
                            P.add("pe", lambda e, DEN=DEN, Pj=Pj, i2=i2, n=len(ul): e.matmul(DEN[:, 0:384], lhsT=ones_b, rhs=Pj, start=(i2 == 0), stop=(i2 == n - 1)),
                                  ["ones", ("P3", j)], [denk])
                        dt_ = V(o_dt[p3], F32, [384]); of3 = V(o_of3[p3], F32, [3, 128])
                        for hh in range(3):
                            P.add("dve", lambda e, DEN=DEN, dt_=dt_, hh=hh, kvh=kvh: e.tensor_scalar(out=dt_[:, hh * 128:(hh + 1) * 128], in0=DEN[:, hh * 128:(hh + 1) * 128], scalar1=esink[:, 3 * kvh + hh:3 * kvh + hh + 1], scalar2=None, op0=ALU.add),
                                  [denk, "esink"], [("dt", p3)])
                        P.add("dve", lambda e, dt_=dt_: e.reciprocal(out=dt_, in_=dt_), [("dt", p3)], [("dt", p3)])
                        P.add("dve", lambda e, ACO=ACO, dt_=dt_, of3=of3: e.tensor_tensor(out=of3, in0=ACO[:, 0:384].rearrange("p (a b) -> p a b", a=3), in1=dt_.rearrange("p (a b) -> p a b", a=3), op=ALU.mult),
                              [acok, ("dt", p3)], [("of3", p3)])
                        P.add("dve", lambda e, of3=of3, Z3=Z3, O3=O3, t=t: e.tensor_tensor(out=O3[:, :, t * 128:(t + 1) * 128], in0=of3, in1=Z3[:, :, t * 128:(t + 1) * 128], op=ALU.mult),
                              [("of3", p3), ("z", p2)], [("o3", p2)])
                    dma(ATT.ap()[12 + 3 * kvh:15 + 3 * kvh].rearrange("c p t -> p c t"), O3, [("o3", p2)], [])
                P.barrier()

                if STOP == 'SW': stopped[0] = True; break
                aM = Alloc(PBASE)
                o_q2 = [aM.get(2 * L * 2) for _ in range(2)]; o_z2 = [aM.get(2 * L * 2) for _ in range(2)]; o_o2 = [aM.get(2 * L * 2) for _ in range(2)]
                o_pm = [aM.get(512 * 2) for _ in range(2)]; o_rdm = aM.get(512 * 4); o_ofm = aM.get(512 * 4)
                for hm in range(4):
                    p2 = hm % 2
                    Q2 = V(o_q2[p2], BF16, [2, L]); Z2 = V(o_z2[p2], BF16, [2, L]); O2 = V(o_o2[p2], BF16, [2, L])
                    dma(Q2, PROJ.ap()[80 + 2 * hm:82 + 2 * hm].rearrange("c p t -> p c t"), writes=[("q", p2)])
                    dma(Z2, PROJ.ap()[88 + 2 * hm:90 + 2 * hm].rearrange("c p t -> p c t"), writes=[("z", p2)])
                    for qb in range(L // 512):
                        qs = slice(qb * 512, (qb + 1) * 512)
                        for mt in range(2):
                            Sb, sk = banks[mt], ("bank", mt)
                            for dc in range(2):
                                P.add("pe", lambda e, Sb=Sb, hm=hm, dc=dc, mt=mt, Q2=Q2, qs=qs: e.matmul(Sb[:, 0:512], lhsT=KmT[:, 2 * hm + dc, mt * 128:(mt + 1) * 128], rhs=Q2[:, dc, qs], start=(dc == 0), stop=(dc == 1)),
                                      ["KmT", ("q", p2)], [sk])
                            Pm = V(o_pm[mt], BF16, [512])
                            P.add("act", lambda e, Sb=Sb, Pm=Pm: e.activation(out=Pm, in_=Sb[:, 0:512], func=AF.Exp), [sk], [("Pm", mt)])
                        for dvc in range(2):
                            Ob, ok_ = banks[2 + dvc], ("bank", 2 + dvc)
                            for mt in range(2):
                                Pm = V(o_pm[mt], BF16, [512])
                                P.add("pe", lambda e, Ob=Ob, hm=hm, dvc=dvc, mt=mt, Pm=Pm: e.matmul(Ob[:, 0:512], lhsT=Vm[:, mt, (2 * hm + dvc) * 128:(2 * hm + dvc + 1) * 128], rhs=Pm, start=(mt == 0), stop=(mt == 1)),
                                      ["Vm", ("Pm", mt)], [ok_])
                        for mt in range(2):
                            Pm = V(o_pm[mt], BF16, [512])
                            P.add("pe", lambda e, mt=mt, Pm=Pm: e.matmul(banks[4][:, 0:512], lhsT=ones_b, rhs=Pm, start=(mt == 0), stop=(mt == 1)), ["ones", ("Pm", mt)], [("bank", 4)])
                        rdm = V(o_rdm, F32, [512]); ofm = V(o_ofm, F32, [512])
                        P.add("dve", lambda e, rdm=rdm: e.reciprocal(out=rdm, in_=banks[4][:, 0:512]), [("bank", 4)], ["rdm"])
                        for dvc in range(2):
                            P.add("dve", lambda e, dvc=dvc, rdm=rdm, ofm=ofm: e.tensor_tensor(out=ofm, in0=banks[2 + dvc][:, 0:512], in1=rdm, op=ALU.mult), [("bank", 2 + dvc), "rdm"], ["ofm"])
                            P.add("dve", lambda e, dvc=dvc, ofm=ofm, Z2=Z2, O2=O2, qs=qs: e.tensor_tensor(out=O2[:, dvc, qs], in0=ofm, in1=Z2[:, dvc, qs], op=ALU.mult), ["ofm", ("z", p2)], [("o2", p2)])
                    dma(ATT.ap()[24 + 2 * hm:26 + 2 * hm].rearrange("c p t -> p c t"), O2, [("o2", p2)], [])
                P.barrier()

                if STOP == 'MEM': stopped[0] = True; break
                aT = Alloc(PBASE)
                o_at = aT.get(32 * 256 * 2); o_mg = aT.get(KC * 256 * 2); o_wbr = [aT.get(32 * 128 * 2) for _ in range(2)]
                o_gt = [aT.get(3 * 8 * 256 * 2)] * 2; o_wo = [aT.get(KC * 256 * 2) for _ in range(2)]
                o_yb = [aT.get(D * 4) for _ in range(2)]; o_xr = aT.get(D * 4); o_t1 = aT.get(256 * 4); o_t2 = aT.get(256 * 4); o_jk = aT.get(256 * 2)
                o_ssq = aT.get(2 * NOG * 4); o_rs2 = aT.get(16)
                assert aT.o <= ARENA_ELEMS * 2, aT.o
                ydst = X1 if l == 0 else y_out
                for b2 in range(cfg.NB2):
                    ts = slice(b2 * 256, (b2 + 1) * 256)
                    AT = V(o_at, BF16, [32, 256]); MG = V(o_mg, BF16, [KC, 256])
                    dma(AT, ATT.ap()[:, :, ts].rearrange("c p t -> p c t"), writes=["AT"])
                    t1 = V(o_t1, F32, [256]); t2 = V(o_t2, F32, [256])
                    for n in range(KC):
                        if n % 8 == 0:
                            gi = 0; nn = min(8, KC - n)
                            GT = V(o_gt[gi], BF16, [3, 8, 256])
                            for ty in range(3):
                                c0 = 96 + ty * KC + n
                                dma(GT[:, ty, 0:nn, :], PROJ.ap()[c0:c0 + nn, :, ts].rearrange("c p t -> p c t"), writes=[("GT", gi)])
                        wi = n % 2
                        wbr = V(o_wbr[wi], BF16, [32, 128])
                        dma(wbr, WBR.ap()[l, n], writes=[("wbr", wi)])
                        bsel = (n % 2) * 3
                        for ty, (k0, k1) in enumerate(((0, 12), (12, 24), (24, 32))):
                            bk, bkk = banks[bsel + ty], ("bank", bsel + ty)
                            for kc in range(k0, k1):
                                P.add("pe", lambda e, bk=bk, wbr=wbr, kc=kc, k0=k0, k1=k1: e.matmul(bk[:, 0:256], lhsT=wbr[:, kc, :], rhs=AT[:, kc, :], start=(kc == k0), stop=(kc == k1 - 1)),
                                      [("wbr", wi), "AT"], [bkk])
                        g8 = n % 8
                        P.add("dve", lambda e, bsel=bsel, GT=GT, g8=g8: e.tensor_tensor(out=t1, in0=banks[bsel][:, 0:256], in1=GT[:, 0, g8, :], op=ALU.mult), [("bank", bsel), ("GT", gi)], ["t1"])
                        P.add("dve", lambda e, bsel=bsel, GT=GT, g8=g8: e.tensor_tensor(out=t2, in0=banks[bsel + 1][:, 0:256], in1=GT[:, 1, g8, :], op=ALU.mult), [("bank", bsel + 1), ("GT", gi)], ["t2"])
                        P.add("pool", lambda e: e.tensor_tensor(out=t1, in0=t1, in1=t2, op=ALU.add), ["t1", "t2"], ["t1"])
                        P.add("dve", lambda e, bsel=bsel, GT=GT, g8=g8: e.tensor_tensor(out=t2, in0=banks[bsel + 2][:, 0:256], in1=GT[:, 2, g8, :], op=ALU.mult), [("bank", bsel + 2), ("GT", gi), "t1"], ["t2"])
                        P.add("pool", lambda e, n=n, MG=MG: e.tensor_tensor(out=MG[:, n, :], in0=t1, in1=t2, op=ALU.add), ["t1", "t2"], ["MG"])
                    ssq = V(o_ssq, F32, [2, NOG]); jk = V(o_jk, BF16, [256])
                    TST = int(os.environ.get('K_TST', '9'))
                    if TST < 2: continue
                    P.add("dve", lambda e, ssq=ssq: e.memset(ssq, 0.0), [], ["ssq"])
                    for og in range(NOG):
                        wi = og % 2
                        wo = V(o_wo[wi], BF16, [KC, 256])
                        dma(wo, WOUT.ap()[l, og], writes=[("wo", wi)])
                        for i in range(2):
                            bk, bkk = nextbank()
                            for kc in range(KC):
                                P.add("pe", lambda e, bk=bk, MG=MG, wo=wo, kc=kc, i=i: e.matmul(bk[:, 0:256], lhsT=MG[:, kc, i * 128:(i + 1) * 128], rhs=wo[:, kc, :], start=(kc == 0), stop=(kc == KC - 1)),
                                      ["MG", ("wo", wi)], [bkk])
                            YB = V(o_yb[i], F32, [D])
                            P.add("dve", lambda e, bk=bk, YB=YB, og=og: e.tensor_copy(out=YB[:, og * 256:(og + 1) * 256], in_=bk[:, 0:256]), [bkk], [("yb", i)])
                            P.add("act", lambda e, YB=YB, jk=jk, ssq=ssq, i=i, og=og: e.activation(out=jk, in_=YB[:, og * 256:(og + 1) * 256], func=AF.Square, accum_out=ssq[:, i, og:og + 1]), [("yb", i), "ssq"], ["ssq", "jk"])
                    rs2 = V(o_rs2, F32, [4])
                    if TST < 3: continue
                    for i in range(2):
                        YB = V(o_yb[i], F32, [D]); XR = V(o_xr, F32, [D])
                        r0 = b2 * 256 + i * 128
                        P.add("dve", lambda e, ssq=ssq, rs2=rs2, i=i: e.tensor_reduce(out=rs2[:, i:i + 1], in_=ssq[:, i, :], axis=mybir.AxisListType.X, op=ALU.add), ["ssq"], [("rs2", i)])
                        P.add("dve", lambda e, rs2=rs2, i=i: e.tensor_scalar(out=rs2[:, i:i + 1], in0=rs2[:, i:i + 1], scalar1=1.0 / D, scalar2=EPS, op0=ALU.mult, op1=ALU.add), [("rs2", i)], [("rs2", i)])
                        P.add("act", lambda e, rs2=rs2, i=i: e.activation(out=rs2[:, i:i + 1], in_=rs2[:, i:i + 1], func=AF.Sqrt), [("rs2", i)], [("rs2", i)])
                        P.add("dve", lambda e, rs2=rs2, i=i: e.reciprocal(out=rs2[:, i:i + 1], in_=rs2[:, i:i + 1]), [("rs2", i)], [("rs2", i)])
                        dma(XR, xsrc.ap()[s, r0:r0 + 128, :], writes=["xr"])
                        P.add("dve", lambda e, YB=YB, rs2=rs2, i=i: e.scalar_tensor_tensor(out=YB, in0=YB, scalar=rs2[:, i:i + 1], in1=gpost, op0=ALU.mult, op1=ALU.mult), [("yb", i), ("rs2", i), "gpost"], [("yb", i)])
                        P.add("pool", lambda e, YB=YB, XR=XR: e.tensor_tensor(out=YB, in0=YB, in1=XR, op=ALU.add), [("yb", i), "xr"], [("yb", i)])
                        dma(ydst.ap()[s, r0:r0 + 128, :], YB, [("yb", i)], [])
                P.barrier()
                if STOP == 'T': stopped[0] = True; break
            if stopped[0]: break

        P.finalize(NSEMD)
        with nc.Block() as block:
            @block.sync
            def _(e):
                P.emit("sp", e, csem, dsem)

            @block.tensor
            def _(e):
                P.emit("pe", e, csem, dsem)

            @block.scalar
            def _(e):
                P.emit("act", e, csem, dsem)

            @block.vector
            def _(e):
                P.emit("dve", e, csem, dsem)

            @block.gpsimd
            def _(e):
                P.emit("pool", e, csem, dsem)
    return nc


def make_in_maps(cfg, xs, mems, pre_norm, post_norm, mem_norm, w_in, w_mem_kv, w_branch_a, w_branch_b, w_branch_m, w_out,
                 na_rpb, attn_sink, t5_bias):
    KC, D, NS = cfg.KC, cfg.D, cfg.NS
    ident, oh, cm16 = host_consts()
    f = lambda a: np.ascontiguousarray(np.asarray(a, dtype=np.float32))
    gl = lambda g: f(np.asarray(g).reshape(2, KC, 128).transpose(0, 2, 1))
    shared = {
        "gpre": gl(pre_norm), "gmem": gl(mem_norm),
        "gpost": f(np.broadcast_to(np.asarray(post_norm)[:, None, :], (2, 128, D))),
        "w_in": f(w_in), "w_kv": f(w_mem_kv), "w_a": f(w_branch_a), "w_b": f(w_branch_b), "w_m": f(w_branch_m), "w_o": f(w_out),
        "rpb": f(np.broadcast_to(np.asarray(na_rpb).reshape(2, 12, 1, 465), (2, 12, 128, 465))),
        "sink": f(np.broadcast_to(np.asarray(attn_sink)[:, None, :], (2, 128, 12))),
        "t5": f(t5_bias), "ident": ident, "oh": oh, "cm16": cm16,
    }
    maps = []
    for c in range(cfg.NCORES):
        m = dict(shared)
        m["x"] = f(xs[c * NS:(c + 1) * NS]); m["mem"] = f(mems[c * NS:(c + 1) * NS])
        maps.append(m)
    return maps


def run(cfg, xs, mems, **w):
    nc = build(cfg)
    maps = make_in_maps(cfg, xs, mems, **w)
    res = run_bass_kernel_spmd(nc, maps, core_ids=list(range(cfg.NCORES)))
    if os.environ.get("K_DBG"):
        run.dbg = res.results
    return np.concatenate([r["y"] for r in res.results], axis=0)


NCORES_FULL = 6


def kernel(x_prompt, x_sample, mem_prompt, mem_sample, pre_norm, post_norm, mem_norm, w_in, w_mem_kv,
           w_branch_a, w_branch_b, w_branch_m, w_out, na_rpb, attn_sink, t5_bias):
    xs = np.concatenate([np.asarray(x_prompt), np.asarray(x_sample)], axis=0)
    mems = np.concatenate([np.asarray(mem_prompt), np.asarray(mem_sample)], axis=0)
    nseq = xs.shape[0]
    cfg = Cfg(xs.shape[2], nseq // NCORES_FULL, NCORES_FULL, xs.shape[1])
    y = run(cfg, xs, mems, pre_norm=pre_norm, post_norm=post_norm, mem_norm=mem_norm, w_in=w_in, w_mem_kv=w_mem_kv,
            w_branch_a=w_branch_a, w_branch_b=w_branch_b, w_branch_m=w_branch_m, w_out=w_out,
            na_rpb=na_rpb, attn_sink=attn_sink, t5_bias=t5_bias)
    nb = np.asarray(x_prompt).shape[0]
    return (np.ascontiguousarray(y[:nb]), np.ascontiguousarray(y[nb:]))
```

```python
import os
import numpy as np
import concourse.bass as bass
import concourse.mybir as mybir
from concourse.bass_utils import run_bass_kernel_spmd

F32 = mybir.dt.float32
BF16 = mybir.dt.bfloat16
AF = mybir.ActivationFunctionType
ALU = mybir.AluOpType
NEG = -30000.0
EPS = 1e-6


class Cfg:
    def __init__(self, D, NS, NCORES, L=2048):
        self.D = D; self.KC = D // 128; self.NS = NS; self.NCORES = NCORES; self.L = L
        self.NT = L // 128; self.R = L // 64
        self.INW = 12288 + 3 * D; self.NCH = self.INW // 128
        self.GW = 256; self.NG = self.INW // 256
        self.OGW = 256; self.NOG = D // 256
        self.TB = 512; self.NB = L // 512
        self.TB2 = 256; self.NB2 = L // 256


def t5_buckets(rel):
    nb = 16; max_exact = 8
    ret = (rel > 0).astype(np.int32) * nb
    n = np.abs(rel)
    large = max_exact + (np.log(np.maximum(n, 1) / max_exact) / np.log(128 / max_exact) * (nb - max_exact)).astype(np.int32)
    large = np.minimum(large, nb - 1)
    return (ret + np.where(n < max_exact, n, large)).astype(np.int32)


def host_consts():
    ident = np.eye(128, dtype=np.float32)
    rel = np.arange(511) - 255
    bk = t5_buckets(rel)
    valid = np.abs(rel) <= 128
    oh = np.zeros((33, 511), np.float32)
    for r in range(511):
        if valid[r]:
            oh[bk[r], r] = 1.0
        else:
            oh[32, r] = NEG
    qc = np.arange(64)
    cstart = np.clip(qc - 8, 0, 48)
    kc = np.arange(64)
    cm = np.where((kc[None, :] >= cstart[:, None]) & (kc[None, :] < cstart[:, None] + 16), 0.0, NEG).astype(np.float32)
    cm128 = np.concatenate([cm, cm], axis=0)
    cm16 = np.ascontiguousarray(np.broadcast_to(cm128[:, None, :], (128, 16, 64))).reshape(128, 1024)
    return ident, oh, cm16


class Op:
    __slots__ = ("eng", "fn", "deps", "dma", "sig", "cnt", "sem", "target", "prev_target")


class Prog:
    CE = ("pe", "act", "dve", "pool")

    def __init__(self):
        self.ops = []
        self.last_w = {}
        self.rd_eng = {}
        self.rd_dma = {}
        self.last_on = {}
        self.dmas_since_bar = []
        self.pending_bar = {}

    def add(self, eng, fn, reads=(), writes=(), dma=False):
        op = Op(); op.eng = eng; op.fn = fn; op.dma = dma; op.sig = False
        idx = len(self.ops)
        deps = set()
        for k in reads:
            w = self.last_w.get(k)
            if w is not None:
                deps.add(w)
        for k in writes:
            w = self.last_w.get(k)
            if w is not None:
                deps.add(w)
            for v in self.rd_eng.get(k, {}).values():
                deps.add(v)
            for v in self.rd_dma.get(k, ()):
                deps.add(v)
        pb = self.pending_bar.pop(eng, None)
        if pb:
            deps |= pb
        deps.discard(idx)
        op.deps = deps
        for k in reads:
            if dma:
                self.rd_dma.setdefault(k, []).append(idx)
            else:
                self.rd_eng.setdefault(k, {})[eng] = idx
        for k in writes:
            self.last_w[k] = idx
            self.rd_eng[k] = {}
            self.rd_dma[k] = []
        self.ops.append(op)
        if dma:
            self.dmas_since_bar.append(idx)
        else:
            self.last_on[eng] = idx
        return idx

    def barrier(self):
        s = set(self.last_on.values()) | set(self.dmas_since_bar)
        for e in ("pe", "act", "dve", "pool", "sp"):
            self.pending_bar[e] = set(s) | self.pending_bar.get(e, set())
        self.dmas_since_bar = []

    def finalize(self, nsem_dma=24):
        ops = self.ops
        for op in ops:
            for d in op.deps:
                ops[d].sig = True
        cnt = {e: 0 for e in self.CE}
        ndma = 0
        for op in ops:
            if op.dma:
                op.sem = ndma % nsem_dma
                op.target = 16 * (ndma // nsem_dma + 1)
                ndma += 1
            elif op.sig:
                cnt[op.eng] += 1
                op.cnt = cnt[op.eng]
        self.nsem_dma = nsem_dma
        self.ndma = ndma

    def emit(self, eng_name, eng, csem, dsem):
        ops = self.ops
        seen = {}
        for op in ops:
            if op.eng != eng_name:
                continue
            need = {}
            for d in op.deps:
                o = ops[d]
                if o.dma:
                    key = ("d", o.sem); val = o.target
                else:
                    if o.eng == eng_name and eng_name == "pe":
                        continue
                    key = ("c", o.eng); val = o.cnt
                if need.get(key, 0) < val:
                    need[key] = val
            if op.dma and op.target > 16:
                key = ("d", op.sem)
                if need.get(key, 0) < op.target - 16:
                    need[key] = op.target - 16
            for key, val in need.items():
                if seen.get(key, 0) >= val:
                    continue
                seen[key] = val
                sem = dsem[key[1]] if key[0] == "d" else csem[key[1]]
                eng.wait_ge(sem, val)
            ins = op.fn(eng)
            if op.dma:
                ins.then_inc(dsem[op.sem], 16)
            elif op.sig:
                ins.then_inc(csem[op.eng], 1)
        if eng_name == "sp":
            for s in range(min(self.nsem_dma, self.ndma)):
                n = (self.ndma - 1 - s) // self.nsem_dma + 1
                if seen.get(("d", s), 0) < 16 * n:
                    eng.wait_ge(dsem[s], 16 * n)


def chunk_kind(c):
    if c < 12: return "qs"
    if c < 36: return "cp"
    if c < 48: return "silu"
    if c < 60: return "qs"
    if c < 68: return "cp"
    if c < 80: return "silu"
    if c < 88: return "qm"
    if c < 96: return "silu"
    return "sig"


def build(cfg):
    D, KC, NS, L, NT, R = cfg.D, cfg.KC, cfg.NS, cfg.L, cfg.NT, cfg.R
    INW, NCH, NG, NOG = cfg.INW, cfg.NCH, cfg.NG, cfg.NOG
    nc = bass.Bass("TRN2", target_bir_lowering=False)

    def din(name, shape, dt=F32):
        return nc.dram_tensor(name, list(shape), dt, kind="ExternalInput")

    x_in = din("x", [NS, L, D]); mem_in = din("mem", [NS, 256, D])
    gpre_in = din("gpre", [2, 128, KC]); gmem_in = din("gmem", [2, 128, KC]); gpost_in = din("gpost", [2, 128, D])
    w_in = din("w_in", [2, D, INW]); w_kv = din("w_kv", [2, D, 2048])
    w_a = din("w_a", [2, 1536, D]); w_b = din("w_b", [2, 1536, D]); w_m = din("w_m", [2, 1024, D]); w_o = din("w_o", [2, D, D])
    rpb_in = din("rpb", [2, 12, 128, 465]); sink_in = din("sink", [2, 128, 12]); t5_in = din("t5", [32, 12])
    ident_in = din("ident", [128, 128]); oh_in = din("oh", [33, 511]); cm_in = din("cm16", [128, 1024])
    y_out = nc.dram_tensor("y", [NS, L, D], F32, kind="ExternalOutput")

    WINL = [nc.dram_tensor(f"s_win{i}", [NG, 128, KC, 256], BF16) for i in range(2)]
    WKV = nc.dram_tensor("s_wkv", [2, 8, 128, KC, 256], BF16)
    WBR = nc.dram_tensor("s_wbr", [2, KC, 128, 32, 128], BF16)
    WOUT = nc.dram_tensor("s_wout", [2, NOG, 128, KC, 256], BF16)
    dbgk = {"kind": "ExternalOutput"} if os.environ.get("K_DBG") else {}
    X1 = nc.dram_tensor("s_x1", [NS, L, D], F32, **dbgk)
    PROJ = nc.dram_tensor("s_proj", [NCH, 128, L], BF16, **dbgk)
    ATT = nc.dram_tensor("s_att", [32, 128, L], BF16, **dbgk)
    FVS = nc.dram_tensor("s_fvs", [12, 128, 511], F32)
    RPD = nc.dram_tensor("s_rpd", [12, 15, 64, 127], F32)

    P = Prog()
    STOP = os.environ.get('K_STOP', '')
    stopped = [False]
    ARENA_ELEMS = 103 * 1024

    from contextlib import ExitStack
    with ExitStack() as es:
        arena_t = es.enter_context(nc.sbuf_tensor("arena", [128, ARENA_ELEMS], BF16))
        banks = [es.enter_context(nc.psum_tensor(f"pb{i}", [128, 512], F32)) for i in range(6)]
        ptb = [es.enter_context(nc.psum_tensor(f"pt{i}", [128, 1024], BF16)) for i in range(2)]
        csem = {e: es.enter_context(nc.semaphore(f"c_{e}")) for e in Prog.CE}
        NSEMD = 24
        dsem = [es.enter_context(nc.semaphore(f"d_{i}")) for i in range(NSEMD)]

        def V(off, dt, shape, parts=128, p0=0):
            n = int(np.prod(shape)); esz = 2 if dt is BF16 else 4
            assert off % 4 == 0
            assert off + n * esz <= ARENA_ELEMS * 2, (off, n, esz)
            a = arena_t[p0:p0 + parts, off // 2: off // 2 + n * esz // 2]
            if dt is not BF16:
                a = a.bitcast(dt)
            if len(shape) == 2:
                a = a.rearrange("p (a b) -> p a b", a=shape[0])
            elif len(shape) == 3:
                a = a.rearrange("p (a b c) -> p a b c", a=shape[0], b=shape[1])
            return a

        class Alloc:
            def __init__(self, base): self.o = base
            def get(self, nbytes):
                o = self.o; self.o += (nbytes + 31) // 32 * 32; return o

        pa = Alloc(0)
        o_ident = pa.get(256); o_ones = pa.get(256); o_ET = pa.get(12 * 3 * 128 * 2); o_TBq = pa.get(12 * 16 * 64 * 2)
        o_gpre = pa.get(KC * 4); o_gmem = pa.get(KC * 4); o_gpost = pa.get(D * 4); o_esink = pa.get(48)
        o_small = pa.get(64 * 4); o_KmT = pa.get(8 * 256 * 2); o_Vm = pa.get(2 * 1024 * 2)
        PBASE = pa.o
        ident_b = V(o_ident, BF16, [128]); ones_b = V(o_ones, BF16, [128])
        ET = V(o_ET, BF16, [12, 3, 128]); TBq = V(o_TBq, BF16, [12, 16, 64])
        gpre = V(o_gpre, F32, [KC]); gmem = V(o_gmem, F32, [KC]); gpost = V(o_gpost, F32, [D]); esink = V(o_esink, F32, [12])
        small = V(o_small, F32, [64])
        KmT = V(o_KmT, BF16, [8, 256]); Vm = V(o_Vm, BF16, [2, 1024])

        def dma(out, in_, reads=(), writes=()):
            P.add("sp", lambda e, out=out, in_=in_: e.dma_start(out=out, in_=in_), reads, writes, dma=True)

        a0 = Alloc(PBASE)
        o_idf = a0.get(512); o_t33 = a0.get(48); o_oh = a0.get(511 * 4); o_l33 = a0.get(512); o_on33 = a0.get(512)
        o_w511 = a0.get(2 * 511 * 4); o_et32 = a0.get(2 * 384 * 4)
        idf = V(o_idf, F32, [128]); t33 = V(o_t33, F32, [12], parts=33); ohs = V(o_oh, F32, [511], parts=33)
        l33 = V(o_l33, F32, [128], parts=33); on33 = V(o_on33, F32, [128], parts=33)
        dma(idf, ident_in.ap(), writes=["idf"])
        P.add("dve", lambda e: e.tensor_copy(out=ident_b, in_=idf), ["idf"], ["ident"])
        P.add("dve", lambda e: e.memset(ones_b, 1.0), [], ["ones"])
        P.add("dve", lambda e: e.memset(t33, 1.0), [], ["t33"])
        P.add("dve", lambda e: e.memset(on33, 1.0), [], ["on33"])
        dma(V(o_t33, F32, [12], parts=32), t5_in.ap(), writes=["t33"])
        dma(ohs, oh_in.ap(), writes=["oh"])
        for h in range(12):
            w511 = V(o_w511 + (h % 2) * 511 * 4, F32, [511]); et32 = V(o_et32 + (h % 2) * 384 * 4, F32, [3, 128])
            kw, ke = ("w511", h % 2), ("et32", h % 2)
            P.add("dve", lambda e, h=h: e.tensor_scalar(out=l33, in0=on33, scalar1=t33[:, h:h + 1], scalar2=None, op0=ALU.mult), ["on33", "t33"], ["l33"])
            P.add("pe", lambda e: e.matmul(banks[0][:, 0:511], lhsT=l33, rhs=ohs, start=True, stop=True), ["l33", "oh"], ["b0"])
            P.add("dve", lambda e, w511=w511: e.tensor_copy(out=w511, in_=banks[0][:, 0:511]), ["b0"], [kw])
            dma(FVS.ap()[h], w511, [kw], [("fvs", h)])
            src = bass.AP(FVS, h * 128 * 511 + 127, [[510, 128], [128, 3], [1, 128]])
            dma(et32, src, [("fvs", h)], [ke])
            P.add("dve", lambda e, h=h, et32=et32: e.tensor_copy(out=ET[:, h, :, :], in_=et32), [ke], ["ET"])
        P.barrier()
        if STOP == 'setup': stopped[0] = True

        cvt_jobs = []
        cvt_split = [0]

        def add_cvt(src2d, K, N, gw, dst_fn):
            CW = min(N, 2048)
            for kc in range(K // 128):
                for c0 in range(0, N, CW):
                    cw = min(CW, N - c0)
                    cvt_jobs.append((src2d[kc * 128:(kc + 1) * 128, c0:c0 + cw], cw, gw, dst_fn(kc, c0 // gw, cw // gw)))

        for l in range(2):
            if l == 1:
                cvt_split[0] = len(cvt_jobs)
            add_cvt(w_in.ap()[l], D, INW, 256, lambda kc, g0, ng, l=l: WINL[l].ap()[g0:g0 + ng, :, kc, :].rearrange("g p w -> p g w"))
            add_cvt(w_kv.ap()[l], D, 2048, 256, lambda kc, g0, ng, l=l: WKV.ap()[l, g0:g0 + ng, :, kc, :].rearrange("g p w -> p g w"))
            for wsrc, kk, koff in ((w_a, 1536, 0), (w_b, 1536, 12), (w_m, 1024, 24)):
                add_cvt(wsrc.ap()[l], kk, D, 128, lambda kc, g0, ng, l=l, koff=koff: WBR.ap()[l, g0:g0 + ng, :, koff + kc, :].rearrange("g p w -> p g w"))
            add_cvt(w_o.ap()[l], D, D, 256, lambda kc, g0, ng, l=l: WOUT.ap()[l, g0:g0 + ng, :, kc, :].rearrange("g p w -> p g w"))
        class Cvt:
            def __init__(self, jobs, base, ncb, engs, tag):
                self.jobs = jobs; self.ncb = ncb; self.engs = engs; self.tag = tag
                self.nl = 0; self.npr = 0
                self.set_base(base)

            def set_base(self, base):
                a = Alloc(base)
                self.o_in = [a.get(2048 * 4) for _ in range(self.ncb)]
                self.o_out = [a.get(2048 * 2) for _ in range(self.ncb)]
                assert a.o <= ARENA_ELEMS * 2
                self.end = a.o

            def load(self):
                if self.nl >= len(self.jobs):
                    return
                i = self.nl; self.nl += 1
                src, CW, gw, dst = self.jobs[i]
                dma(V(self.o_in[i % self.ncb], F32, [CW]), src, writes=[(self.tag + "in", i % self.ncb)])

            def start(self):
                while self.nl < min(len(self.jobs), self.npr + self.ncb):
                    self.load()

            def step(self, refill=True):
                if self.npr >= self.nl:
                    return False
                i = self.npr; self.npr += 1
                src, CW, gw, dst = self.jobs[i]
                sl = i % self.ncb
                cin = V(self.o_in[sl], F32, [CW]); cout = V(self.o_out[sl], BF16, [CW])
                en = self.engs[i % len(self.engs)]
                kin, kout = (self.tag + "in", sl), (self.tag + "out", sl)
                if en == "act":
                    P.add("act", lambda e, cin=cin, cout=cout: e.copy(out=cout, in_=cin), [kin], [kout])
                else:
                    P.add(en, lambda e, cin=cin, cout=cout: e.tensor_copy(out=cout, in_=cin), [kin], [kout])
                dma(dst, V(self.o_out[sl], BF16, [CW // gw, gw]), [kout], [])
                if refill:
                    self.load()
                return True

            def drain(self):
                while self.step(refill=False):
                    pass

            def run_all(self):
                self.start()
                while self.step():
                    pass

            def done(self):
                return self.npr >= len(self.jobs)

        if stopped[0]:
            cvt_jobs = []; cvt_split[0] = 0
        Cvt(cvt_jobs[:cvt_split[0]], PBASE, 6, ("dve", "act", "pool"), "c0").run_all()
        bg = Cvt(cvt_jobs[cvt_split[0]:], PBASE, 6, ("pool", "act"), "c1")
        P.barrier()
        if STOP == 'cvt': stopped[0] = True

        def rms_tile(src_dram, xt, junk, hb, ssc, tag):
            ss, rs = ssc
            dma(xt, src_dram, writes=[("xt", tag)])
            P.add("dve", lambda e: e.memset(ss, 0.0), [], [("ss", tag)])
            P.add("act", lambda e: e.activation(out=junk, in_=xt, func=AF.Square, accum_out=ss), [("xt", tag), ("ss", tag)], [("ss", tag), ("hb", tag)])
            P.add("dve", lambda e: e.tensor_scalar(out=rs, in0=ss, scalar1=1.0 / D, scalar2=EPS, op0=ALU.mult, op1=ALU.add), [("ss", tag)], [("rs", tag)])
            P.add("act", lambda e: e.activation(out=rs, in_=rs, func=AF.Sqrt), [("rs", tag)], [("rs", tag)])
            P.add("dve", lambda e: e.reciprocal(out=rs, in_=rs), [("rs", tag)], [("rs", tag)])
            P.add("dve", lambda e: e.tensor_scalar(out=hb, in0=xt, scalar1=rs, scalar2=None, op0=ALU.mult), [("xt", tag), ("rs", tag)], [("hb", tag)])

        tcount = [0]

        def transpose_to(hb, tag, gain, dstT, col0):
            for k0 in range(0, KC, 8):
                nk = min(8, KC - k0)
                pt = ptb[tcount[0] % 2]; pk = ("pt", tcount[0] % 2); tcount[0] += 1
                for j in range(nk):
                    kc = k0 + j
                    P.add("pe", lambda e, pt=pt, j=j, kc=kc: e.transpose(out=pt[:, j * 128:(j + 1) * 128], in_=hb[:, kc * 128:(kc + 1) * 128], identity=ident_b),
                          [("hb", tag), "ident"], [pk])
                for j in range(nk):
                    kc = k0 + j
                    if j % 2 == 0:
                        P.add("act", lambda e, pt=pt, j=j, kc=kc: e.activation(out=dstT[:, kc, col0:col0 + 128], in_=pt[:, j * 128:(j + 1) * 128], func=AF.Copy, scale=gain[:, kc:kc + 1]),
                              [pk], [("hT", col0)])
                    else:
                        P.add("dve", lambda e, pt=pt, j=j, kc=kc: e.tensor_scalar(out=dstT[:, kc, col0:col0 + 128], in0=pt[:, j * 128:(j + 1) * 128], scalar1=gain[:, kc:kc + 1], scalar2=None, op0=ALU.mult),
                              [pk], [("hT", col0)])

        aA = Alloc(PBASE)
        o_xt = [aA.get(D * 4) for _ in range(2)]; o_hb = [aA.get(D * 2) for _ in range(2)]
        o_hT = aA.get(KC * 512 * 2); o_wb = [aA.get(KC * 256 * 2) for _ in range(3)]; o_stg = [aA.get(4 * 512 * 2) for _ in range(2)]
        assert aA.o <= ARENA_ELEMS * 2, aA.o
        wcount = [0]

        def wload(src):
            i = wcount[0] % 3; wcount[0] += 1
            wb = V(o_wb[i], BF16, [KC, 256])
            dma(wb, src, writes=[("wb", i)])
            return wb, ("wb", i)

        bcount = [0]

        def nextbank(n=6):
            i = bcount[0] % n; bcount[0] += 1
            return banks[i], ("bank", i)

        for l in range(2):
            if stopped[0]: break
            if l == 1 and not bg.done():
                bg.set_base(PBASE); bg.run_all(); P.barrier()
            dma(gpre, gpre_in.ap()[l], writes=["gpre"]); dma(gmem, gmem_in.ap()[l], writes=["gmem"])
            dma(gpost, gpost_in.ap()[l], writes=["gpost"]); dma(esink, sink_in.ap()[l], writes=["esink"])
            P.add("act", lambda e: e.activation(out=esink, in_=esink, func=AF.Exp), ["esink"], ["esink"])
            aT = Alloc(PBASE)
            o_rpb = aT.get(15 * 127 * 4); o_tb32 = aT.get(16 * 64 * 4); o_cm = aT.get(1024 * 4)
            rpbt = V(o_rpb, F32, [15, 127]); tb32 = V(o_tb32, F32, [16, 64]); cmt = V(o_cm, F32, [16, 64])
            dma(cmt, cm_in.ap().rearrange("p (a b) -> p a b", a=16), writes=["cm"])
            P.add("dve", lambda e: e.memset(rpbt, NEG), [], ["rpbt"])
            for h in range(12):
                dma(rpbt[:, :, 48:79], rpb_in.ap()[l, h].rearrange("p (a b) -> p a b", a=15), writes=["rpbt"])
                dma(RPD.ap()[h].rearrange("d r w -> r d w"), V(o_rpb, F32, [15, 127], parts=64), ["rpbt"], [("rpd", h)])
                P.add("dve", lambda e: e.memset(tb32, NEG), [], ["tb32"])
                base = h * 15 * 64 * 127
                dma(V(o_tb32, F32, [16, 64], parts=64)[:, 1:16, :], bass.AP(RPD, base + 63, [[126, 64], [64 * 127, 15], [1, 64]]), [("rpd", h)], ["tb32"])
                dma(V(o_tb32, F32, [16, 64], parts=64, p0=64)[:, 2:16, :], bass.AP(RPD, base + 63, [[126, 64], [64 * 127, 14], [1, 64]]), [("rpd", h)], ["tb32"])
                P.add("dve", lambda e, h=h: e.tensor_tensor(out=TBq[:, h, :, :], in0=tb32, in1=cmt, op=ALU.add), ["tb32", "cm"], ["TBq"])
            P.barrier()
            if STOP == 'tables': stopped[0] = True; break

            for s in range(NS):
                memhT = V(o_hT, BF16, [KC, 256])
                for i in range(2):
                    xt = V(o_xt[i], F32, [D]); hb = V(o_hb[i], BF16, [D])
                    rms_tile(mem_in.ap()[s, i * 128:(i + 1) * 128, :], xt, hb, hb, (small[:, i:i + 1], small[:, 8 + i:9 + i]), i)
                    transpose_to(hb, i, gmem, memhT, i * 128)
                hkeys = [("hT", 0), ("hT", 128)]
                for g in range(8):
                    wb, wk = wload(WKV.ap()[l, g])
                    if g < 4:
                        for j in range(2):
                            bk, bkk = nextbank()
                            for kc in range(KC):
                                P.add("pe", lambda e, bk=bk, wb=wb, j=j, kc=kc: e.matmul(bk[:, 0:256], lhsT=wb[:, kc, j * 128:(j + 1) * 128], rhs=memhT[:, kc, :], start=(kc == 0), stop=(kc == KC - 1)),
                                      [wk] + hkeys, [bkk])
                            P.add("dve", lambda e, bk=bk, g=g, j=j: e.tensor_copy(out=KmT[:, 2 * g + j, :], in_=bk[:, 0:256]), [bkk], ["KmT"])
                    else:
                        for mt in range(2):
                            bk, bkk = nextbank()
                            for kc in range(KC):
                                P.add("pe", lambda e, bk=bk, wb=wb, mt=mt, kc=kc: e.matmul(bk[:, 0:256], lhsT=memhT[:, kc, mt * 128:(mt + 1) * 128], rhs=wb[:, kc, :], start=(kc == 0), stop=(kc == KC - 1)),
                                      [wk] + hkeys, [bkk])
                            P.add("dve", lambda e, bk=bk, g=g, mt=mt: e.tensor_copy(out=Vm[:, mt, (g - 4) * 256:(g - 3) * 256], in_=bk[:, 0:256]), [bkk], ["Vm"])
                P.barrier()

                if STOP == 'M': stopped[0] = True; break
                xsrc = x_in if l == 0 else X1
                for b in range(cfg.NB):
                    hT = V(o_hT, BF16, [KC, 512])
                    for i in range(4):
                        xt = V(o_xt[i % 2], F32, [D]); hb = V(o_hb[i % 2], BF16, [D])
                        r0 = b * 512 + i * 128
                        rms_tile(xsrc.ap()[s, r0:r0 + 128, :], xt, hb, hb, (small[:, i:i + 1], small[:, 8 + i:9 + i]), i % 2)
                        transpose_to(hb, i % 2, gpre, hT, i * 128)
                    hkeys = [("hT", i * 128) for i in range(4)]
                    pend = [wload(WINL[l].ap()[g]) for g in range(min(2, NG))]
                    for g in range(NG):
                        wb, wk = pend.pop(0)
                        for j in range(2):
                            c = 2 * g + j
                            bk, bkk = nextbank()
                            for kc in range(KC):
                                P.add("pe", lambda e, bk=bk, wb=wb, j=j, kc=kc: e.matmul(bk[:, 0:512], lhsT=wb[:, kc, j * 128:(j + 1) * 128], rhs=hT[:, kc, :], start=(kc == 0), stop=(kc == KC - 1)),
                                      [wk] + hkeys, [bkk])
                            si = (c // 4) % 2
                            stg = V(o_stg[si], BF16, [4, 512])
                            dst = stg[:, c % 4, :]
                            kind = chunk_kind(c)
                            if kind == "cp":
                                P.add("dve", lambda e, bk=bk, dst=dst: e.tensor_copy(out=dst, in_=bk[:, 0:512]), [bkk], [("stg", si)])
                            elif kind == "qs":
                                P.add("dve", lambda e, bk=bk, dst=dst: e.tensor_scalar(out=dst, in0=bk[:, 0:512], scalar1=128.0 ** -0.5, scalar2=None, op0=ALU.mult), [bkk], [("stg", si)])
                            elif kind == "qm":
                                P.add("dve", lambda e, bk=bk, dst=dst: e.tensor_scalar(out=dst, in0=bk[:, 0:512], scalar1=1.0 / 16.0, scalar2=None, op0=ALU.mult), [bkk], [("stg", si)])
                            elif kind == "silu":
                                P.add("act", lambda e, bk=bk, dst=dst: e.activation(out=dst, in_=bk[:, 0:512], func=AF.Silu), [bkk], [("stg", si)])
                            else:
                                P.add("act", lambda e, bk=bk, dst=dst: e.activation(out=dst, in_=bk[:, 0:512], func=AF.Sigmoid), [bkk], [("stg", si)])
                            if c % 4 == 3 or c == NCH - 1:
                                cb = c - (c % 4)
                                dma(PROJ.ap()[cb:c + 1, :, b * 512:(b + 1) * 512].rearrange("c p t -> p c t"), stg[:, 0:c - cb + 1, :], [("stg", si)], [])
                        if g + 2 < NG:
                            pend.append(wload(WINL[l].ap()[g + 2]))
                P.barrier()

                if STOP == 'A': stopped[0] = True; break
                aN = Alloc(PBASE)
                o_q = [aN.get(L * 2) for _ in range(2)]; o_k = [aN.get(L * 2) for _ in range(2)]; o_v = [aN.get(L * 2) for _ in range(2)]
                o_z = [aN.get(L * 2) for _ in range(2)]; o_vt = [aN.get(L * 2) for _ in range(2)]; o_oh_ = [aN.get(L * 2) for _ in range(2)]
                o_pt = [aN.get(5 * 128 * 2) for _ in range(2)]; o_rd = [aN.get(128 * 4) for _ in range(2)]; o_of = [aN.get(128 * 4) for _ in range(2)]

                if l == 0:
                    bg.set_base(aN.o); bg.start()

                def vtrans(VT, vk, Vt, vtk):
                    for half in range(2):
                        pt = ptb[tcount[0] % 2]; pk = ("pt", tcount[0] % 2); tcount[0] += 1
                        for j in range(8):
                            u = half * 8 + j
                            P.add("pe", lambda e, pt=pt, j=j, u=u: e.transpose(out=pt[:, j * 128:(j + 1) * 128], in_=VT[:, u * 128:(u + 1) * 128], identity=ident_b), [vk, "ident"], [pk])
                        P.add("dve", lambda e, pt=pt, half=half: e.tensor_copy(out=Vt[:, half * 1024:(half + 1) * 1024], in_=pt[:, 0:1024]), [pk], [vtk])

                def rs_of(r):
                    return min(max(r - 4, 0), R - 8)
                for h in range(12):
                    p2 = h % 2
                    QT = V(o_q[p2], BF16, [L]); KT = V(o_k[p2], BF16, [L]); VT = V(o_v[p2], BF16, [L]); ZT = V(o_z[p2], BF16, [L])
                    Vt = V(o_vt[p2], BF16, [L]); OH = V(o_oh_[p2], BF16, [L])
                    dma(QT, PROJ.ap()[h], writes=[("q", p2)]); dma(KT, PROJ.ap()[12 + h], writes=[("k", p2)])
                    dma(VT, PROJ.ap()[24 + h], writes=[("v", p2)]); dma(ZT, PROJ.ap()[36 + h], writes=[("z", p2)])
                    vtrans(VT, ("v", p2), Vt, ("vt", p2))
                    def na_iter(t, part, h=h, p2=p2, QT=QT, KT=KT, ZT=ZT, Vt=Vt, OH=OH):
                        ul = []
                        for u in range(NT):
                            inv = []
                            for kr in range(2):
                                for qr in range(2):
                                    rs = rs_of(2 * t + qr)
                                    if not (rs <= 2 * u + kr <= rs + 7):
                                        inv.append((kr, qr))
                            if len(inv) < 4:
                                ul.append((u, inv))
                        assert 1 <= len(ul) <= 5
                        it = h * NT + t; p3 = it % 2
                        S0, s0k = banks[2 * p3], ("bank", 2 * p3)
                        S1, s1k = banks[2 * p3 + 1], ("bank", 2 * p3 + 1)
                        ACC, acck = banks[4 + p3], ("bank", 4 + p3)
                        Pt = V(o_pt[p3], BF16, [5, 128]); ptk = ("Pt", p3)
                        for idx, (u, inv) in enumerate(ul if part == "front" else []):
                            Sb, sk, col = (S0, s0k, idx * 128) if idx < 4 else (S1, s1k, 0)
                            e0 = 2 * (u - t) + 8
                            assert 0 <= e0 <= 14
                            P.add("pe", lambda e, Sb=Sb, col=col, u=u, t=t, KT=KT, QT=QT: e.matmul(Sb[:, col:col + 128], lhsT=KT[:, u * 128:(u + 1) * 128], rhs=QT[:, t * 128:(t + 1) * 128], start=True, stop=False),
                                  [("k", p2), ("q", p2)], [sk])
                            P.add("pe", lambda e, Sb=Sb, col=col, h=h, e0=e0: e.matmul(Sb[:, col:col + 128], lhsT=TBq[:, h, e0:e0 + 2, :], rhs=ident_b, start=False, stop=True),
                                  ["TBq", "ident"], [sk])
                        if part == "front":
                            return
                        n0 = min(4, len(ul))
                        P.add("act", lambda e, S0=S0, Pt=Pt, n0=n0: e.activation(out=Pt[:, 0:n0, :], in_=S0[:, 0:n0 * 128].rearrange("p (a b) -> p a b", a=n0), func=AF.Exp), [s0k], [ptk])
                        if len(ul) == 5:
                            P.add("act", lambda e, S1=S1, Pt=Pt: e.activation(out=Pt[:, 4, :], in_=S1[:, 0:128], func=AF.Exp), [s1k], [ptk])
                        for idx, (u, inv) in enumerate(ul):
                            for (kr, qr) in inv:
                                Pz = V(o_pt[p3], BF16, [5, 128], parts=64, p0=kr * 64)
                                P.add("dve", lambda e, Pz=Pz, idx=idx, qr=qr: e.memset(Pz[:, idx, qr * 64:(qr + 1) * 64], 0.0), [], [ptk])
                        nu = len(ul)
                        for idx, (u, inv) in enumerate(ul):
                            P.add("pe", lambda e, ACC=ACC, Vt=Vt, Pt=Pt, u=u, idx=idx, nu=nu: e.matmul(ACC[:, 0:128], lhsT=Vt[:, u * 128:(u + 1) * 128], rhs=Pt[:, idx, :], start=(idx == 0), stop=(idx == nu - 1)),
                                  [("vt", p2), ptk], [acck])
                        for idx, (u, inv) in enumerate(ul):
                            P.add("pe", lambda e, ACC=ACC, Pt=Pt, idx=idx, nu=nu: e.matmul(ACC[:, 128:256], lhsT=ones_b, rhs=Pt[:, idx, :], start=(idx == 0), stop=(idx == nu - 1)),
                                  ["ones", ptk], [acck])
                        rd = V(o_rd[p3], F32, [128]); of = V(o_of[p3], F32, [128])
                        P.add("dve", lambda e, ACC=ACC, rd=rd: e.reciprocal(out=rd, in_=ACC[:, 128:256]), [acck], [("rd", p3)])
                        P.add("dve", lambda e, ACC=ACC, rd=rd, of=of: e.tensor_tensor(out=of, in0=ACC[:, 0:128], in1=rd, op=ALU.mult), [acck, ("rd", p3)], [("of", p3)])
                        P.add("dve", lambda e, of=of, ZT=ZT, OH=OH, t=t: e.tensor_tensor(out=OH[:, t * 128:(t + 1) * 128], in0=of, in1=ZT[:, t * 128:(t + 1) * 128], op=ALU.mult),
                              [("of", p3), ("z", p2)], [("oh", p2)])
                    na_iter(0, "front")
                    for t in range(NT):
                        if t + 1 < NT:
                            na_iter(t + 1, "front")
                        na_iter(t, "back")
                        if l == 0:
                            bg.step()
                            if t % 2:
                                bg.step()
                    dma(ATT.ap()[h], OH, [("oh", p2)], [])
                if l == 0:
                    bg.drain()
                P.barrier()

                if STOP == 'NA': stopped[0] = True; break
                aS = Alloc(PBASE)
                o_k = [aS.get(L * 2) for _ in range(2)]; o_v = [aS.get(L * 2) for _ in range(2)]; o_vt = [aS.get(L * 2) for _ in range(2)]
                o_q3 = [aS.get(3 * L * 2) for _ in range(2)]; o_z3 = [aS.get(3 * L * 2) for _ in range(2)]; o_o3 = [aS.get(3 * L * 2) for _ in range(2)]
                o_p3 = [aS.get(384 * 2) for _ in range(3)]; o_dt = [aS.get(384 * 4) for _ in range(2)]; o_of3 = [aS.get(384 * 4) for _ in range(2)]
                assert aS.o <= ARENA_ELEMS * 2
                for kvh in range(4):
                    p2 = kvh % 2
                    KT = V(o_k[p2], BF16, [L]); VT = V(o_v[p2], BF16, [L]); Vt = V(o_vt[p2], BF16, [L])
                    Q3 = V(o_q3[p2], BF16, [3, L]); Z3 = V(o_z3[p2], BF16, [3, L]); O3 = V(o_o3[p2], BF16, [3, L])
                    dma(KT, PROJ.ap()[60 + kvh], writes=[("k", p2)]); dma(VT, PROJ.ap()[64 + kvh], writes=[("v", p2)])
                    dma(Q3, PROJ.ap()[48 + 3 * kvh:51 + 3 * kvh].rearrange("c p t -> p c t"), writes=[("q", p2)])
                    dma(Z3, PROJ.ap()[68 + 3 * kvh:71 + 3 * kvh].rearrange("c p t -> p c t"), writes=[("z", p2)])
                    vtrans(VT, ("v", p2), Vt, ("vt", p2))
                    for t in range(NT):
                        ul = [u for u in (t - 1, t, t + 1) if 0 <= u < NT]
                        it = kvh * NT + t; p3 = it % 2
                        ACO, acok = banks[3], ("bank", 3)
                        DEN, denk = banks[4], ("bank", 4)
                        if p3:
                            ACO, acok = banks[5], ("bank", 5)
                        for u in ul:
                            j = u - t + 1
                            Sb, sk = banks[j], ("bank", j)
                            P.add("pe", lambda e, Sb=Sb, KT=KT, Q3=Q3, u=u, t=t: e.matmul(Sb[:, 0:384], lhsT=KT[:, u * 128:(u + 1) * 128], rhs=Q3[:, :, t * 128:(t + 1) * 128], start=True, stop=False),
                                  [("k", p2), ("q", p2)], [sk])
                            for hh in range(3):
                                P.add("pe", lambda e, Sb=Sb, hh=hh, kvh=kvh, j=j: e.matmul(Sb[:, hh * 128:(hh + 1) * 128], lhsT=ET[:, 3 * kvh + hh, j, :], rhs=ident_b, start=False, stop=(hh == 2)),
                                      ["ET", "ident"], [sk])
                            Pj = V(o_p3[j], BF16, [384])
                            P.add("act", lambda e, Sb=Sb, Pj=Pj: e.activation(out=Pj, in_=Sb[:, 0:384], func=AF.Exp), [sk], [("P3", j)])
                        for i2, u in enumerate(ul):
                            j = u - t + 1; Pj = V(o_p3[j], BF16, [384])
                            P.add("pe", lambda e, ACO=ACO, Vt=Vt, Pj=Pj, u=u, i2=i2, n=len(ul): e.matmul(ACO[:, 0:384], lhsT=Vt[:, u * 128:(u + 1) * 128], rhs=Pj, start=(i2 == 0), stop=(i2 == n - 1)),
                                  [("vt", p2), ("P3", j)], [acok])
                        for i2, u in enumerate(ul):
                            j = u - t + 1; Pj = V(o_p3[j], BF16, [384])
                            P.add("pe", lambda e, DEN=DEN, Pj=Pj, i2=i2, n=len(ul): e.matmul(DEN[:, 0:384], lhsT=ones_b, rhs=Pj, start=(i2 == 0), stop=(i2 == n - 1)),
                                  ["ones", ("P3", j)], [denk])
                        dt_ = V(o_dt[p3], F32, [384]); of3 = V(o_of3[p3], F32, [3, 128])
                        for hh in range(3):
                            P.add("dve", lambda e, DEN=DEN, dt_=dt_, hh=hh, kvh=kvh: e.tensor_scalar(out=dt_[:, hh * 128:(hh + 1) * 128], in0=DEN[:, hh * 128:(hh + 1) * 128], scalar1=esink[:, 3 * kvh + hh:3 * kvh + hh + 1], scalar2=None, op0=ALU.add),
                                  [denk, "esink"], [("dt", p3)])
                        P.add("dve", lambda e, dt_=dt_: e.reciprocal(out=dt_, in_=dt_), [("dt", p3)], [("dt", p3)])
                        P.add("dve", lambda e, ACO=ACO, dt_=dt_, of3=of3: e.tensor_tensor(out=of3, in0=ACO[:, 0:384].rearrange("p (a b) -> p a b", a=3), in1=dt_.rearrange("p (a b) -> p a b", a=3), op=ALU.mult),
                              [acok, ("dt", p3)], [("of3", p3)])
                        P.add("dve", lambda e, of3=of3, Z3=Z3, O3=O3, t=t: e.tensor_tensor(out=O3[:, :, t * 128:(t + 1) * 128], in0=of3, in1=Z3[:, :, t * 128:(t + 1) * 128], op=ALU.mult),
                              [("of3", p3), ("z", p2)], [("o3", p2)])
                    dma(ATT.ap()[12 + 3 * kvh:15 + 3 * kvh].rearrange("c p t -> p c t"), O3, [("o3", p2)], [])
                P.barrier()

                if STOP == 'SW': stopped[0] = True; break
                aM = Alloc(PBASE)
                o_q2 = [aM.get(2 * L * 2) for _ in range(2)]; o_z2 = [aM.get(2 * L * 2) for _ in range(2)]; o_o2 = [aM.get(2 * L * 2) for _ in range(2)]
                o_pm = [aM.get(512 * 2) for _ in range(2)]; o_rdm = aM.get(512 * 4); o_ofm = aM.get(512 * 4)
                for hm in range(4):
                    p2 = hm % 2
                    Q2 = V(o_q2[p2], BF16, [2, L]); Z2 = V(o_z2[p2], BF16, [2, L]); O2 = V(o_o2[p2], BF16, [2, L])
                    dma(Q2, PROJ.ap()[80 + 2 * hm:82 + 2 * hm].rearrange("c p t -> p c t"), writes=[("q", p2)])
                    dma(Z2, PROJ.ap()[88 + 2 * hm:90 + 2 * hm].rearrange("c p t -> p c t"), writes=[("z", p2)])
                    for qb in range(L // 512):
                        qs = slice(qb * 512, (qb + 1) * 512)
                        for mt in range(2):
                            Sb, sk = banks[mt], ("bank", mt)
                            for dc in range(2):
                                P.add("pe", lambda e, Sb=Sb, hm=hm, dc=dc, mt=mt, Q2=Q2, qs=qs: e.matmul(Sb[:, 0:512], lhsT=KmT[:, 2 * hm + dc, mt * 128:(mt + 1) * 128], rhs=Q2[:, dc, qs], start=(dc == 0), stop=(dc == 1)),
                                      ["KmT", ("q", p2)], [sk])
                            Pm = V(o_pm[mt], BF16, [512])
                            P.add("act", lambda e, Sb=Sb, Pm=Pm: e.activation(out=Pm, in_=Sb[:, 0:512], func=AF.Exp), [sk], [("Pm", mt)])
                        for dvc in range(2):
                            Ob, ok_ = banks[2 + dvc], ("bank", 2 + dvc)
                            for mt in range(2):
                                Pm = V(o_pm[mt], BF16, [512])
                                P.add("pe", lambda e, Ob=Ob, hm=hm, dvc=dvc, mt=mt, Pm=Pm: e.matmul(Ob[:, 0:512], lhsT=Vm[:, mt, (2 * hm + dvc) * 128:(2 * hm + dvc + 1) * 128], rhs=Pm, start=(mt == 0), stop=(mt == 1)),
                                      ["Vm", ("Pm", mt)], [ok_])
                        for mt in range(2):
                            Pm = V(o_pm[mt], BF16, [512])
                            P.add("pe", lambda e, mt=mt, Pm=Pm: e.matmul(banks[4][:, 0:512], lhsT=ones_b, rhs=Pm, start=(mt == 0), stop=(mt == 1)), ["ones", ("Pm", mt)], [("bank", 4)])
                        rdm = V(o_rdm, F32, [512]); ofm = V(o_ofm, F32, [512])
                        P.add("dve", lambda e, rdm=rdm: e.reciprocal(out=rdm, in_=banks[4][:, 0:512]), [("bank", 4)], ["rdm"])
                        for dvc in range(2):
                            P.add("dve", lambda e, dvc=dvc, rdm=rdm, ofm=ofm: e.tensor_tensor(out=ofm, in0=banks[2 + dvc][:, 0:512], in1=rdm, op=ALU.mult), [("bank", 2 + dvc), "rdm"], ["ofm"])
                            P.add("dve", lambda e, dvc=dvc, ofm=ofm, Z2=Z2, O2=O2, qs=qs: e.tensor_tensor(out=O2[:, dvc, qs], in0=ofm, in1=Z2[:, dvc, qs], op=ALU.mult), ["ofm", ("z", p2)], [("o2", p2)])
                    dma(ATT.ap()[24 + 2 * hm:26 + 2 * hm].rearrange("c p t -> p c t"), O2, [("o2", p2)], [])
                P.barrier()

                if STOP == 'MEM': stopped[0] = True; break
                aT = Alloc(PBASE)
                o_at = aT.get(32 * 256 * 2); o_mg = aT.get(KC * 256 * 2); o_wbr = [aT.get(32 * 128 * 2) for _ in range(2)]
                o_gt = [aT.get(3 * 8 * 256 * 2)] * 2; o_wo = [aT.get(KC * 256 * 2) for _ in range(2)]
                o_yb = [aT.get(D * 4) for _ in range(2)]; o_xr = aT.get(D * 4); o_t1 = aT.get(256 * 4); o_t2 = aT.get(256 * 4); o_jk = aT.get(256 * 2)
                o_ssq = aT.get(2 * NOG * 4); o_rs2 = aT.get(16)
                assert aT.o <= ARENA_ELEMS * 2, aT.o
                ydst = X1 if l == 0 else y_out
                for b2 in range(cfg.NB2):
                    ts = slice(b2 * 256, (b2 + 1) * 256)
                    AT = V(o_at, BF16, [32, 256]); MG = V(o_mg, BF16, [KC, 256])
                    dma(AT, ATT.ap()[:, :, ts].rearrange("c p t -> p c t"), writes=["AT"])
                    t1 = V(o_t1, F32, [256]); t2 = V(o_t2, F32, [256])
                    for n in range(KC):
                        if n % 8 == 0:
                            gi = 0; nn = min(8, KC - n)
                            GT = V(o_gt[gi], BF16, [3, 8, 256])
                            for ty in range(3):
                                c0 = 96 + ty * KC + n
                                dma(GT[:, ty, 0:nn, :], PROJ.ap()[c0:c0 + nn, :, ts].rearrange("c p t -> p c t"), writes=[("GT", gi)])
                        wi = n % 2
                        wbr = V(o_wbr[wi], BF16, [32, 128])
                        dma(wbr, WBR.ap()[l, n], writes=[("wbr", wi)])
                        bsel = (n % 2) * 3
                        for ty, (k0, k1) in enumerate(((0, 12), (12, 24), (24, 32))):
                            bk, bkk = banks[bsel + ty], ("bank", bsel + ty)
                            for kc in range(k0, k1):
                                P.add("pe", lambda e, bk=bk, wbr=wbr, kc=kc, k0=k0, k1=k1: e.matmul(bk[:, 0:256], lhsT=wbr[:, kc, :], rhs=AT[:, kc, :], start=(kc == k0), stop=(kc == k1 - 1)),
                                      [("wbr", wi), "AT"], [bkk])
                        g8 = n % 8
                        P.add("dve", lambda e, bsel=bsel, GT=GT, g8=g8: e.tensor_tensor(out=t1, in0=banks[bsel][:, 0:256], in1=GT[:, 0, g8, :], op=ALU.mult), [("bank", bsel), ("GT", gi)], ["t1"])
                        P.add("dve", lambda e, bsel=bsel, GT=GT, g8=g8: e.tensor_tensor(out=t2, in0=banks[bsel + 1][:, 0:256], in1=GT[:, 1, g8, :], op=ALU.mult), [("bank", bsel + 1), ("GT", gi)], ["t2"])
                        P.add("pool", lambda e: e.tensor_tensor(out=t1, in0=t1, in1=t2, op=ALU.add), ["t1", "t2"], ["t1"])
                        P.add("dve", lambda e, bsel=bsel, GT=GT, g8=g8: e.tensor_tensor(out=t2, in0=banks[bsel + 2][:, 0:256], in1=GT[:, 2, g8, :], op=ALU.mult), [("bank", bsel + 2), ("GT", gi), "t1"], ["t2"])
                        P.add("pool", lambda e, n=n, MG=MG: e.tensor_tensor(out=MG[:, n, :], in0=t1, in1=t2, op=ALU.add), ["t1", "t2"], ["MG"])
                    ssq = V(o_ssq, F32, [2, NOG]); jk = V(o_jk, BF16, [256])
                    TST = int(os.environ.get('K_TST', '9'))
                    if TST < 2: continue
                    P.add("dve", lambda e, ssq=ssq: e.memset(ssq, 0.0), [], ["ssq"])
                    for og in range(NOG):
                        wi = og % 2
                        wo = V(o_wo[wi], BF16, [KC, 256])
                        dma(wo, WOUT.ap()[l, og], writes=[("wo", wi)])
                        for i in range(2):
                            bk, bkk = nextbank()
                            for kc in range(KC):
                                P.add("pe", lambda e, bk=bk, MG=MG, wo=wo, kc=kc, i=i: e.matmul(bk[:, 0:256], lhsT=MG[:, kc, i * 128:(i + 1) * 128], rhs=wo[:, kc, :], start=(kc == 0), stop=(kc == KC - 1)),
                                      ["MG", ("wo", wi)], [bkk])
                            YB = V(o_yb[i], F32, [D])
                            P.add("dve", lambda e, bk=bk, YB=YB, og=og: e.tensor_copy(out=YB[:, og * 256:(og + 1) * 256], in_=bk[:, 0:256]), [bkk], [("yb", i)])
                            P.add("act", lambda e, YB=YB, jk=jk, ssq=ssq, i=i, og=og: e.activation(out=jk, in_=YB[:, og * 256:(og + 1) * 256], func=AF.Square, accum_out=ssq[:, i, og:og + 1]), [("yb", i), "ssq"], ["ssq", "jk"])
                    rs2 = V(o_rs2, F32, [4])
                    if TST < 3: continue
                    for i in range(2):
                        YB = V(o_yb[i], F32, [D]); XR = V(o_xr, F32, [D])
                        r0 = b2 * 256 + i * 128
                        P.add("dve", lambda e, ssq=ssq, rs2=rs2, i=i: e.tensor_reduce(out=rs2[:, i:i + 1], in_=ssq[:, i, :], axis=mybir.AxisListType.X, op=ALU.add), ["ssq"], [("rs2", i)])
                        P.add("dve", lambda e, rs2=rs2, i=i: e.tensor_scalar(out=rs2[:, i:i + 1], in0=rs2[:, i:i + 1], scalar1=1.0 / D, scalar2=EPS, op0=ALU.mult, op1=ALU.add), [("rs2", i)], [("rs2", i)])
                        P.add("act", lambda e, rs2=rs2, i=i: e.activation(out=rs2[:, i:i + 1], in_=rs2[:, i:i + 1], func=AF.Sqrt), [("rs2", i)], [("rs2", i)])
                        P.add("dve", lambda e, rs2=rs2, i=i: e.reciprocal(out=rs2[:, i:i + 1], in_=rs2[:, i:i + 1]), [("rs2", i)], [("rs2", i)])
                        dma(XR, xsrc.ap()[s, r0:r0 + 128, :], writes=["xr"])
                        P.add("dve", lambda e, YB=YB, rs2=rs2, i=i: e.scalar_tensor_tensor(out=YB, in0=YB, scalar=rs2[:, i:i + 1], in1=gpost, op0=ALU.mult, op1=ALU.mult), [("yb", i), ("rs2", i), "gpost"], [("yb", i)])
                        P.add("pool", lambda e, YB=YB, XR=XR: e.tensor_tensor(out=YB, in0=YB, in1=XR, op=ALU.add), [("yb", i), "xr"], [("yb", i)])
                        dma(ydst.ap()[s, r0:r0 + 128, :], YB, [("yb", i)], [])
                P.barrier()
                if STOP == 'T': stopped[0] = True; break
            if stopped[0]: break

        P.finalize(NSEMD)
        with nc.Block() as block:
            @block.sync
            def _(e):
                P.emit("sp", e, csem, dsem)

            @block.tensor
            def _(e):
                P.emit("pe", e, csem, dsem)

            @block.scalar
            def _(e):
                P.emit("act", e, csem, dsem)

            @block.vector
            def _(e):
                P.emit("dve", e, csem, dsem)

            @block.gpsimd
            def _(e):
                P.emit("pool", e, csem, dsem)
    return nc


def make_in_maps(cfg, xs, mems, pre_norm, post_norm, mem_norm, w_in, w_mem_kv, w_branch_a, w_branch_b, w_branch_m, w_out,
                 na_rpb, attn_sink, t5_bias):
    KC, D, NS = cfg.KC, cfg.D, cfg.NS
    ident, oh, cm16 = host_consts()
    f = lambda a: np.ascontiguousarray(np.asarray(a, dtype=np.float32))
    gl = lambda g: f(np.asarray(g).reshape(2, KC, 128).transpose(0, 2, 1))
    shared = {
        "gpre": gl(pre_norm), "gmem": gl(mem_norm),
        "gpost": f(np.broadcast_to(np.asarray(post_norm)[:, None, :], (2, 128, D))),
        "w_in": f(w_in), "w_kv": f(w_mem_kv), "w_a": f(w_branch_a), "w_b": f(w_branch_b), "w_m": f(w_branch_m), "w_o": f(w_out),
        "rpb": f(np.broadcast_to(np.asarray(na_rpb).reshape(2, 12, 1, 465), (2, 12, 128, 465))),
        "sink": f(np.broadcast_to(np.asarray(attn_sink)[:, None, :], (2, 128, 12))),
        "t5": f(t5_bias), "ident": ident, "oh": oh, "cm16": cm16,
    }
    maps = []
    for c in range(cfg.NCORES):
        m = dict(shared)
        m["x"] = f(xs[c * NS:(c + 1) * NS]); m["mem"] = f(mems[c * NS:(c + 1) * NS])
        maps.append(m)
    return maps


def run(cfg, xs, mems, **w):
    nc = build(cfg)
    maps = make_in_maps(cfg, xs, mems, **w)
    res = run_bass_kernel_spmd(nc, maps, core_ids=list(range(cfg.NCORES)))
    if os.environ.get("K_DBG"):
        run.dbg = res.results
    return np.concatenate([r["y"] for r in res.results], axis=0)


NCORES_FULL = 6


def kernel(x_prompt, x_sample, mem_prompt, mem_sample, pre_norm, post_norm, mem_norm, w_in, w_mem_kv,
           w_branch_a, w_branch_b, w_branch_m, w_out, na_rpb, attn_sink, t5_bias):
    xs = np.concatenate([np.asarray(x_prompt), np.asarray(x_sample)], axis=0)
    mems = np.concatenate([np.asarray(mem_prompt), np.asarray(mem_sample)], axis=0)
    nseq = xs.shape[0]
    cfg = Cfg(xs.shape[2], nseq // NCORES_FULL, NCORES_FULL, xs.shape[1])
    y = run(cfg, xs, mems, pre_norm=pre_norm, post_norm=post_norm, mem_norm=mem_norm, w_in=w_in, w_mem_kv=w_mem_kv,
            w_branch_a=w_branch_a, w_branch_b=w_branch_b, w_branch_m=w_branch_m, w_out=w_out,
            na_rpb=na_rpb, attn_sink=attn_sink, t5_bias=t5_bias)
    nb = np.asarray(x_prompt).shape[0]
    return (np.ascontiguousarray(y[:nb]), np.ascontiguousarray(y[nb:]))
```
